# Optimizing a Trainium2 kernel written in Bass

```python
import jax, jax.numpy as jnp
from jax import lax
import numpy as np

D_MODEL = 1024
BATCH = 4
SEQ = 4096
DEPTH = 1

MIX_WIDTH = D_MODEL
SB_WIDTH = MIX_WIDTH // 2
SB_HEAD_DIM = 64
SB_HEADS = SB_WIDTH // SB_HEAD_DIM
HG_WIDTH = MIX_WIDTH - SB_WIDTH
HG_HEAD_DIM = 128
HG_HEADS = HG_WIDTH // HG_HEAD_DIM
D_FF = 2816
Q_BLOCK = 128
HG_CHUNK = 64
EPS = 1e-6
IN_WIDTHS = (SB_WIDTH, SB_WIDTH, SB_WIDTH, HG_WIDTH, HG_WIDTH, HG_WIDTH, HG_WIDTH)
IN_COLS = sum(IN_WIDTHS)
IN_SPLITS = tuple(int(s) for s in np.cumsum(IN_WIDTHS)[:-1])

kernel_name = "hybrid_stickbreak_hgrn2_macaron"


def rmsnorm(x, gain):
    xf = x.astype(jnp.float32)
    y = xf * lax.rsqrt(jnp.mean(xf * xf, axis=-1, keepdims=True) + EPS)
    return (y * gain.astype(jnp.float32)).astype(x.dtype)


def head_rmsnorm(o, gain, n_heads, head_dim):
    B, T, _ = o.shape
    of = o.astype(jnp.float32).reshape(B, T, n_heads, head_dim)
    of = of * lax.rsqrt(jnp.mean(of * of, axis=-1, keepdims=True) + EPS)
    return of.reshape(B, T, n_heads * head_dim) * gain.astype(jnp.float32)


def swiglu(h, w_gate, w_up, w_down):
    return (jax.nn.silu(h @ w_gate) * (h @ w_up)) @ w_down


def split_heads(a, n_heads, head_dim):
    B, T, _ = a.shape
    return a.reshape(B, T, n_heads, head_dim).transpose(0, 2, 1, 3)


def merge_heads(a):
    B, H, T, Dh = a.shape
    return a.transpose(0, 2, 1, 3).reshape(B, T, H * Dh)


def stick_breaking_attention(q, k, v):
    B, H, T, Dh = q.shape
    n_blocks = T // Q_BLOCK
    scale = Dh ** -0.5
    q_blocks = q.reshape(B, H, n_blocks, Q_BLOCK, Dh).transpose(2, 0, 1, 3, 4)
    key_pos = jnp.arange(T)

    def block(args):
        q_blk, blk_idx = args
        z = jnp.einsum('bhqd,bhkd->bhqk', q_blk, k) * scale
        q_pos = blk_idx * Q_BLOCK + jnp.arange(Q_BLOCK)
        causal = key_pos[None, :] < q_pos[:, None]
        log_beta = jax.nn.log_sigmoid(z)
        log_rest = jnp.where(causal, jax.nn.log_sigmoid(-z), 0.0)
        tail = lax.cumsum(log_rest, axis=3, reverse=True) - log_rest
        w = jnp.where(causal, jnp.exp(log_beta + tail), 0.0)
        return jnp.einsum('bhqk,bhkd->bhqd', w, v)

    out = lax.map(block, (q_blocks, jnp.arange(n_blocks)))
    return out.transpose(1, 2, 0, 3, 4).reshape(B, H, T, Dh)


def hgrn2_chunkwise(q, k, v, log_f):
    B, H, T, Dk = q.shape
    Dv = v.shape[-1]
    n = T // HG_CHUNK

    def to_chunks(a):
        return a.reshape(B, H, n, HG_CHUNK, a.shape[-1]).transpose(2, 0, 1, 3, 4)

    qc, kc, vc = to_chunks(q), to_chunks(k), to_chunks(v)
    bc = lax.cumsum(to_chunks(log_f), axis=3)
    idx = jnp.arange(HG_CHUNK)
    incl = (idx[:, None] >= idx[None, :])[:, :, None]

    def step(S, xs):
        q_c, k_c, v_c, b_c = xs
        diff = b_c[:, :, :, None, :] - b_c[:, :, None, :, :]
        decay = jnp.where(incl, jnp.exp(jnp.minimum(diff, 0.0)), 0.0)
        scores = jnp.einsum('bhtd,bhsd,bhtsd->bhts', q_c, k_c, decay)
        o_intra = jnp.einsum('bhts,bhsv->bhtv', scores, v_c)
        o_inter = jnp.einsum('bhtd,bhdv->bhtv', q_c * jnp.exp(b_c), S)
        b_last = b_c[:, :, -1:, :]
        S_new = jnp.exp(b_last[:, :, 0, :])[..., None] * S + jnp.einsum(
            'bhsd,bhsv->bhdv', k_c * jnp.exp(b_last - b_c), v_c)
        return S_new, o_intra + o_inter

    S0 = jnp.zeros((B, H, Dk, Dv), jnp.float32)
    _, o = lax.scan(step, S0, (qc, kc, vc, bc))
    return o.transpose(1, 2, 0, 3, 4).reshape(B, H, T, Dv)


def setup_inputs(seed: int = 0) -> dict:
    key = jax.random.key(seed)
    ks = jax.random.split(key, 20)
    f32 = jnp.float32

    def normal(k, shape, scale):
        return jax.random.normal(k, shape, f32) * scale

    def gain(k, shape):
        return 1.0 + 0.02 * jax.random.normal(k, shape, f32)

    return {
        "x": jax.random.normal(ks[0], (BATCH, SEQ, D_MODEL), f32),
        "ffn1_norm": gain(ks[1], (DEPTH, D_MODEL)),
        "ffn1_w_gate": normal(ks[2], (DEPTH, D_MODEL, D_FF), D_MODEL ** -0.5),
        "ffn1_w_up": normal(ks[3], (DEPTH, D_MODEL, D_FF), D_MODEL ** -0.5),
        "ffn1_w_down": normal(ks[4], (DEPTH, D_FF, D_MODEL), D_FF ** -0.5),
        "mix_norm": gain(ks[5], (DEPTH, D_MODEL)),
        "w_in": normal(ks[6], (DEPTH, D_MODEL, IN_COLS), D_MODEL ** -0.5),
        "sb_out_norm": gain(ks[7], (DEPTH, SB_WIDTH)),
        "hg_lower_bound_logits": normal(ks[8], (DEPTH + 1, HG_WIDTH), 0.1),
        "hg_out_norm": gain(ks[9], (DEPTH, HG_WIDTH)),
        "w_out": normal(ks[10], (DEPTH, MIX_WIDTH, D_MODEL), MIX_WIDTH ** -0.5),
        "ffn2_norm": gain(ks[11], (DEPTH, D_MODEL)),
        "ffn2_w_gate": normal(ks[12], (DEPTH, D_MODEL, D_FF), D_MODEL ** -0.5),
        "ffn2_w_up": normal(ks[13], (DEPTH, D_MODEL, D_FF), D_MODEL ** -0.5),
        "ffn2_w_down": normal(ks[14], (DEPTH, D_FF, D_MODEL), D_FF ** -0.5),
        "final_norm": gain(ks[15], (D_MODEL,)),
    }


def reference(x, ffn1_norm, ffn1_w_gate, ffn1_w_up, ffn1_w_down, mix_norm, w_in, sb_out_norm,
              hg_lower_bound_logits, hg_out_norm, w_out, ffn2_norm, ffn2_w_gate, ffn2_w_up,
              ffn2_w_down, final_norm):
    f32 = jnp.float32
    lower_bounds = lax.cumsum(jax.nn.softmax(hg_lower_bound_logits.astype(f32), axis=0), axis=0)

    for l in range(DEPTH):
        h = rmsnorm(x, ffn1_norm[l])
        x = x + 0.5 * swiglu(h, ffn1_w_gate[l], ffn1_w_up[l], ffn1_w_down[l])

        h = rmsnorm(x, mix_norm[l])
        proj = h @ w_in[l]
        q_sb, k_sb, v_sb, q_hg, f_hg, i_hg, g_hg = jnp.split(proj, IN_SPLITS, axis=-1)

        o_sb = stick_breaking_attention(split_heads(q_sb.astype(f32), SB_HEADS, SB_HEAD_DIM),
                                        split_heads(k_sb.astype(f32), SB_HEADS, SB_HEAD_DIM),
                                        split_heads(v_sb.astype(f32), SB_HEADS, SB_HEAD_DIM))
        o_sb = head_rmsnorm(merge_heads(o_sb), sb_out_norm[l], SB_HEADS, SB_HEAD_DIM)

        lb = lower_bounds[l]
        forget = lb + (1.0 - lb) * jax.nn.sigmoid(f_hg.astype(f32))
        q_h = jax.nn.silu(q_hg.astype(f32))
        o_hg = hgrn2_chunkwise(split_heads(q_h, HG_HEADS, HG_HEAD_DIM),
                               split_heads(1.0 - forget, HG_HEADS, HG_HEAD_DIM),
                               split_heads(i_hg.astype(f32), HG_HEADS, HG_HEAD_DIM),
                               split_heads(jnp.log(forget), HG_HEADS, HG_HEAD_DIM))
        o_hg = head_rmsnorm(merge_heads(o_hg), hg_out_norm[l], HG_HEADS, HG_HEAD_DIM) * jax.nn.silu(g_hg.astype(f32))

        mixed = jnp.concatenate([o_sb, o_hg], axis=-1).astype(x.dtype)
        x = x + mixed @ w_out[l]

        h = rmsnorm(x, ffn2_norm[l])
        x = x + 0.5 * swiglu(h, ffn2_w_gate[l], ffn2_w_up[l], ffn2_w_down[l])

    return rmsnorm(x, final_norm)
```

```python
import numpy as np
from contextlib import ExitStack
import concourse.bass as bass
import concourse.mybir as mybir
from concourse.bass_utils import run_bass_kernel_spmd

F32 = mybir.dt.float32
BF16 = mybir.dt.bfloat16
AF = mybir.ActivationFunctionType
ALU = mybir.AluOpType
AX = mybir.AxisListType

D = 1024
DFF = 2816
NGRP = 11
EPS = 1e-6
NEG = -30000.0
NFILL = 1
CW = 128 + 128 + 2048 + 1024 + 512
ENGS = ('pe', 'act', 'dve', 'pool', 'sp')
EMAP = {'pe': 'tensor', 'act': 'scalar', 'dve': 'vector', 'pool': 'gpsimd', 'sp': 'sync'}


class Builder:
    def __init__(self, nc, es):
        self.nc = nc
        self.es = es
        self.sems = {}
        self.cnt = {}
        self.streams = {e: [] for e in ENGS}
        self.known = {e: {} for e in ENGS}
        self.state = {}
        self.snap = {}

    def _sem(self, key):
        if key not in self.sems:
            name = "s_" + (key if isinstance(key, str) else "d_" + str(key[1]))
            self.sems[key] = self.es.enter_context(self.nc.semaphore(name))
            self.cnt[key] = 0
        return self.sems[key]

    def _issue(self, eng, semkey, inc, fn, reads, writes):
        deps = {}

        def add(t):
            if t is not None and deps.get(t[0], 0) < t[1]:
                deps[t[0]] = t[1]
        for r in reads:
            st = self.state.get(r)
            if st:
                add(st[0])
        for w in writes:
            st = self.state.get(w)
            if st:
                add(st[0])
                for k, v in st[1].items():
                    add((k, v))
        kn = self.known[eng]
        waits = []
        for k, v in deps.items():
            if k == 'pe' and eng == 'pe':
                continue
            if kn.get(k, 0) >= v:
                continue
            waits.append((k, v))
        for k, v in waits:
            if kn.get(k, 0) < v:
                kn[k] = v
            sn = self.snap.get((k, v))
            if sn:
                for kk, vv in sn.items():
                    if kn.get(kk, 0) < vv:
                        kn[kk] = vv
        self._sem(semkey)
        self.cnt[semkey] += (inc or 1)
        val = self.cnt[semkey]
        self.snap[(semkey, val)] = dict(kn)
        self.streams[eng].append((waits, fn, semkey, inc))
        tok = (semkey, val)
        for w in writes:
            self.state[w] = [tok, {}]
        for r in reads:
            st = self.state.setdefault(r, [None, {}])
            st[1][semkey] = val
        return tok

    def op(self, eng, fn, reads, writes):
        return self._issue(eng, eng, 1, fn, reads, writes)

    def dma(self, queue, fn, name, reads, writes):
        return self._issue(queue, ('d', name), 16, fn, reads, writes)

    def barrier(self, exclude=()):
        for e in ENGS:
            kn = self.known[e]
            waits = []
            for k, c in self.cnt.items():
                if k in exclude:
                    continue
                if c > 0 and kn.get(k, 0) < c:
                    waits.append((k, c))
                    kn[k] = c
            if waits:
                self.streams[e].append((waits, None, None, 0))
        self.state = {}
        self.snap = {}

    def wait_all(self, eng):
        kn = self.known[eng]
        waits = []
        for k, c in self.cnt.items():
            if c > 0 and kn.get(k, 0) < c:
                waits.append((k, c))
                kn[k] = c
        if waits:
            self.streams[eng].append((waits, None, None, 0))

    def flush(self):
        streams = self.streams
        self.streams = {e: [] for e in ENGS}
        sems = self.sems
        with self.nc.Block() as block:
            for e in ENGS:
                lst = streams[e]
                if not lst:
                    continue

                def body(eng, lst=lst):
                    for waits, fn, semkey, inc in lst:
                        for k, v in waits:
                            eng.wait_ge(sems[k], v)
                        if fn is not None:
                            if inc:
                                fn(eng).then_inc(sems[semkey], inc)
                            else:
                                fn(eng).then_inc(sems[semkey])
                getattr(block, EMAP[e])(body)


def idx_layout(NO):
    NT = 2 * NO
    L = {}
    L['A'] = 0
    L['B'] = 8
    L['C'] = L['B'] + NT // 128
    L['NPC'] = NT // 1024
    L['V'] = L['C'] + 2 * L['NPC']
    L['NCH'] = NT // 64
    L['E'] = L['V'] + 2 * L['NCH']
    L['N'] = L['E'] + 12
    return L


def make_idx(r, NO):
    L = idx_layout(NO)
    NT = 2 * NO
    p = np.arange(128)
    ix = np.zeros((128, L['N']), np.int64)
    for hl in range(4):
        for j in range(2):
            ix[:64, L['A'] + hl * 2 + j] = j * 512 + (4 * r + hl) * 64 + p[:64]
    for n in range(NT // 128):
        ix[:, L['B'] + n] = 2 * (n * 128 + p) + r
    for hgl in range(2):
        for pc in range(L['NPC']):
            j = (pc * 1024) // NO
            ix[:, L['C'] + hgl * L['NPC'] + pc] = j * 512 + (2 * r + hgl) * 128 + p
    for hgl in range(2):
        for cg in range(L['NCH']):
            ix[:64, L['V'] + hgl * L['NCH'] + cg] = 4 * (cg * 64 + p[:64]) + 2 * r + hgl
    for h in range(8):
        ix[:64, L['E'] + h] = 2 * (h * 64 + p[:64]) + r
    for hd in range(4):
        ix[:, L['E'] + 8 + hd] = 2 * (hd * 128 + p) + r
    return ix.astype(np.int32)


class _Stop(Exception):
    pass


def build(NO, dbg=False, stop=99):
    NP = 0
    NT = 2 * NO
    PZ = 1024
    assert NO % PZ == 0
    LX = idx_layout(NO)
    nc = bass.Bass("TRN2", target_bir_lowering=False)
    xin = nc.dram_tensor("xin", [NO, D], F32, kind="ExternalInput")
    idxd = nc.dram_tensor("idxd", [128, LX['N']], mybir.dt.int32, kind="ExternalInput")
    gains = nc.dram_tensor("gains", [4, 128, D], F32, kind="ExternalInput")
    cst = nc.dram_tensor("cst", [128, CW], F32, kind="ExternalInput")
    small = nc.dram_tensor("small", [128, 20], F32, kind="ExternalInput")
    w1g = nc.dram_tensor("w1g", [D, DFF], F32, kind="ExternalInput")
    w1u = nc.dram_tensor("w1u", [D, DFF], F32, kind="ExternalInput")
    w1d = nc.dram_tensor("w1d", [DFF, D], F32, kind="ExternalInput")
    w2g = nc.dram_tensor("w2g", [D, DFF], F32, kind="ExternalInput")
    w2u = nc.dram_tensor("w2u", [D, DFF], F32, kind="ExternalInput")
    w2d = nc.dram_tensor("w2d", [DFF, D], F32, kind="ExternalInput")
    win = nc.dram_tensor("win", [D, 3584], F32, kind="ExternalInput")
    wout = nc.dram_tensor("wout", [D, D], F32, kind="ExternalInput")
    out = nc.dram_tensor("out", [NO, D], F32, kind="ExternalOutput")
    SUB = NO // 1024
    SAk = nc.dram_tensor("SAk", [512, NO], BF16)
    SAq = nc.dram_tensor("SAq", [512, NO], BF16)
    GAk = nc.dram_tensor("GAk", [1024, NO], BF16)
    GAq = nc.dram_tensor("GAq", [1024, NO], BF16)
    SBv = nc.dram_tensor("SBv", [NO, 512], BF16)
    SBi = nc.dram_tensor("SBi", [NO, 512], BF16)
    GBv = nc.dram_tensor("GBv", [2 * NO, 512], BF16)
    GBi = nc.dram_tensor("GBi", [2 * NO, 512], BF16)
    SCt = [[nc.dram_tensor(f"SC{k}_{t}", [512, 1024], F32) for t in range(SUB)] for k in range(3)]
    GCt = [[nc.dram_tensor(f"GC{k}_{t}", [1024, 1024], F32) for t in range(SUB)] for k in range(3)]
    S2A = nc.dram_tensor("S2A", [256, NT], BF16)
    S2B = nc.dram_tensor("S2B", [256, NT], BF16)
    G2A = nc.dram_tensor("G2A", [512, NT], BF16)
    G2B = nc.dram_tensor("G2B", [512, NT], BF16)
    PAIRS = [[0, 1], [2, 3], [4, 5], [6, 7]]
    if dbg:
        d_x1 = nc.dram_tensor("d_x1", [NO, D], F32, kind="ExternalOutput")
        d_ohg = nc.dram_tensor("d_ohg", [256, NT], BF16, kind="ExternalOutput")
        d_osb = nc.dram_tensor("d_osb", [256, NT], BF16, kind="ExternalOutput")

    try:
      with ExitStack() as es0:
        bld = Builder(nc, es0)

        def finish(k):
            if stop != k:
                return
            for i in range(NO // 128):
                bld.dma('sp', lambda e, i=i: e.dma_start(out=out[i * 128:(i + 1) * 128, :], in_=res[:, i, :]),
                        f"res{i}", [], [])
            bld.barrier()
            bld.flush()
            raise _Stop()

        uid = [0]

        def sbt(es, name, shape, dt):
            uid[0] += 1
            return es.enter_context(nc.sbuf_tensor(f"{name}_{uid[0]}", shape, dt))

        NTILE = NO // 128
        res = sbt(es0, "res", [128, NTILE, D], F32)
        gt = sbt(es0, "gt", [128, D], F32)
        ident = sbt(es0, "ident", [128, 128], BF16)
        negU = sbt(es0, "negU", [128, 128], BF16)
        negOnes = sbt(es0, "negOnes", [128, 128], BF16)
        maskneg = sbt(es0, "maskneg", [128, 2048], BF16)
        scanmask = sbt(es0, "scanmask", [128, 1024], BF16)
        incl = sbt(es0, "incl", [128, 512], BF16)
        ones128 = sbt(es0, "ones128", [128, 128], F32)
        ones64 = sbt(es0, "ones64", [64, 64], F32)
        smallt = sbt(es0, "smallt", [128, 20], F32)
        idxt = sbt(es0, "idxt", [128, LX['N']], mybir.dt.int32)
        lbt = sbt(es0, "lbt", [128, 4], F32)
        omlt = sbt(es0, "omlt", [128, 4], F32)
        ss = sbt(es0, "ss", [128, 16], F32)
        rstd = sbt(es0, "rstd", [128, 16], F32)
        def alloc_psum(es):
            uid[0] += 1
            u = uid[0]
            f = [es.enter_context(nc.psum_tensor(f"psf{i}_{u}", [128, 512], F32)) for i in range(7)]
            t = es.enter_context(nc.psum_tensor(f"pst_{u}", [128, 1024], BF16))
            return f, t

        bld.dma('pool', lambda e: e.dma_start(out=ident[:, :], in_=cst[:, 0:128]), 'c0', [], ['ident'])
        bld.dma('pool', lambda e: e.dma_start(out=negU[:, :], in_=cst[:, 128:256]), 'c1', [], ['negU'])
        bld.dma('pool', lambda e: e.dma_start(out=maskneg[:, :], in_=cst[:, 256:2304]), 'c2', [], ['maskneg'])
        bld.dma('pool', lambda e: e.dma_start(out=scanmask[:, :], in_=cst[:, 2304:3328]), 'c3', [], ['scanmask'])
        bld.dma('pool', lambda e: e.dma_start(out=incl[:, :], in_=cst[:, 3328:3840]), 'c4', [], ['incl'])
        bld.dma('sp', lambda e: e.dma_start(out=smallt[:, :], in_=small[:, :]), 'c5', [], ['smallt'])
        bld.dma('sp', lambda e: e.dma_start(out=idxt[:, :], in_=idxd[:, :]), 'c6', [], ['idxt'])
        bld.op('dve', lambda e: e.memset(negOnes[:, :], -1.0), [], ['negOnes'])
        bld.op('dve', lambda e: e.memset(ones128[:, :], 1.0 / 128.0), [], ['ones128'])
        bld.op('dve', lambda e: e.memset(ones64[:, :], 1.0 / 64.0), [], ['ones64'])
        bld.op('dve', lambda e: e.tensor_tensor(out=lbt[:, :], in0=smallt[:, 16:20], in1=smallt[:, 12:16], op=ALU.subtract),
               ['smallt'], ['lbt'])
        bld.op('act', lambda e: e.activation(out=lbt[:, :], in_=lbt[:, :], func=AF.Exp), ['lbt'], ['lbt'])
        bld.op('dve', lambda e: e.tensor_scalar(out=lbt[:, :], in0=lbt[:, :], scalar1=1.0, scalar2=None, op0=ALU.add),
               ['lbt'], ['lbt'])
        bld.op('dve', lambda e: e.reciprocal(out=lbt[:, :], in_=lbt[:, :]), ['lbt'], ['lbt'])
        bld.op('dve', lambda e: e.tensor_scalar(out=omlt[:, :], in0=lbt[:, :], scalar1=-1.0, scalar2=1.0,
                                                op0=ALU.mult, op1=ALU.add), ['lbt'], ['omlt'])

        def norm_T(es_bufs, src_row0, gidx):
            hT, xh, sqj = es_bufs
            bld.dma('sp', lambda e: e.dma_start(out=gt[:, :], in_=gains[gidx, :, :]), 'gt', [], ['gt'])
            for i in range(NTILE):
                if src_row0 is not None:
                    bld.dma('sp', lambda e, i=i: e.dma_start(out=res[:, i, :],
                                                              in_=xin[src_row0 + i * 128: src_row0 + (i + 1) * 128, :]),
                            f"res{i}", [], [('res', i)])
                bld.op('act', lambda e, i=i: e.activation(out=sqj[:, :], in_=res[:, i, :], func=AF.Square),
                       [('res', i)], ['sqj'])
                bld.op('dve', lambda e, i=i: e.reduce_sum(out=ss[:, i:i + 1], in_=sqj[:, :], axis=AX.X),
                       ['sqj'], ['ss'])
            bld.op('act', lambda e: e.activation(out=rstd[:, 0:NTILE], in_=ss[:, 0:NTILE], func=AF.Ln, scale=1.0 / D, bias=EPS),
                   ['ss'], ['rstd'])
            bld.op('act', lambda e: e.activation(out=rstd[:, 0:NTILE], in_=rstd[:, 0:NTILE], func=AF.Exp, scale=-0.5),
                   ['rstd'], ['rstd'])
            for i in range(NTILE):
                k = i % 2
                bld.op('dve', lambda e, i=i, k=k: e.scalar_tensor_tensor(
                    out=xh[k][:, :], in0=res[:, i, :], scalar=rstd[:, i:i + 1], in1=gt[:, :],
                    op0=ALU.mult, op1=ALU.mult), [('res', i), 'rstd', 'gt'], [('xh', k)])
                for kc in range(8):
                    bld.op('pe', lambda e, k=k, kc=kc: e.transpose(
                        out=pst[:, kc * 128:(kc + 1) * 128], in_=xh[k][:, kc * 128:(kc + 1) * 128], identity=ident[:, :]),
                        [('xh', k), 'ident'], ['pst'])
                bld.op('act', lambda e, i=i: e.activation(
                    out=hT[:, :, i * 128:(i + 1) * 128], in_=pst[:, :].rearrange("p (k t) -> p k t", k=8), func=AF.Copy),
                    ['pst'], [('hT', i // 4)])

        def ffn_groups(es_bufs, ffn_bufs, wg, wu, wd):
            hT = es_bufs[0]
            wgb, wub, wdb, sgb, aTb = ffn_bufs
            wgv = wg.ap().rearrange("(kc p) f -> p kc f", p=128)
            wuv = wu.ap().rearrange("(kc p) f -> p kc f", p=128)
            ntt = NTILE // 4

            def load(gi):
                s = gi % 2
                f0 = gi * 256
                bld.dma('pool', lambda e: e.dma_start(out=wgb[s][:, :, :], in_=wgv[:, :, f0:f0 + 256]),
                        f"wgb{s}", [], [('wgb', s)])
                bld.dma('pool', lambda e: e.dma_start(out=wub[s][:, :, :], in_=wuv[:, :, f0:f0 + 256]),
                        f"wub{s}", [], [('wub', s)])
                bld.dma('pool', lambda e: e.dma_start(
                    out=wdb[s][:, :, :], in_=wd.ap()[f0:f0 + 256, :].rearrange("(c p) m -> p c m", p=128)),
                    f"wdb{s}", [], [('wdb', s)])

            def gu(gi, tt, ab, c):
                s = gi % 2
                for (wb, wk, pb) in ((wgb, 'wgb', c), (wub, 'wub', 2 + c)):
                    for kc in range(8):
                        bld.op('pe', lambda e, wb=wb, pb=pb, kc=kc: e.matmul(
                            psf[pb][:, :], lhsT=wb[s][:, kc, c * 128:(c + 1) * 128],
                            rhs=hT[:, kc, tt * 512:(tt + 1) * 512], start=(kc == 0), stop=(kc == 7)),
                            [(wk, s), ('hT', tt)], [('psf', pb)])
                bld.op('act', lambda e: e.activation(out=sgb[c][:, :], in_=psf[c][:, :], func=AF.Silu),
                       [('psf', c)], [('sg', c)])
                bld.op('dve', lambda e: e.tensor_tensor(out=aTb[ab][:, c, :], in0=sgb[c][:, :],
                                                        in1=psf[2 + c][:, :], op=ALU.mult),
                       [('sg', c), ('psf', 2 + c)], [('aT', ab, c)])

            ycnt = [0]

            def down(gi, tt, ab, hf):
                s = gi % 2
                for sub in (2 * hf, 2 * hf + 1):
                    ti = tt * 4 + sub
                    for half in range(2):
                        yb = 4 + ycnt[0] % 3
                        ycnt[0] += 1
                        for c in range(2):
                            bld.op('pe', lambda e, yb=yb, c=c, sub=sub, half=half: e.matmul(
                                psf[yb][:, :], lhsT=aTb[ab][:, c, sub * 128:(sub + 1) * 128],
                                rhs=wdb[s][:, c, half * 512:(half + 1) * 512], start=(c == 0), stop=(c == 1)),
                                [('aT', ab, c), ('wdb', s)], [('psf', yb)])
                        bld.op('dve', lambda e, yb=yb, ti=ti, half=half: e.scalar_tensor_tensor(
                            out=res[:, ti, half * 512:(half + 1) * 512], in0=psf[yb][:, :], scalar=0.5,
                            in1=res[:, ti, half * 512:(half + 1) * 512], op0=ALU.mult, op1=ALU.add),
                            [('psf', yb), ('res', ti)], [('res', ti)])

            steps = [(gi, tt) for gi in range(NGRP) for tt in range(ntt)]
            load(0)
            prev = None
            for idx, (gi, tt) in enumerate(steps):
                if tt == min(1, ntt - 1) and gi + 1 < NGRP:
                    load(gi + 1)
                for c in range(2):
                    gu(gi, tt, idx % 2, c)
                    if prev is not None:
                        down(*prev, c)
                prev = (gi, tt, idx % 2)
            down(*prev, 0)
            down(*prev, 1)

        def in_proj(es_bufs, ip_bufs, blocks, hook=None):
            hT = es_bufs[0]
            wib, stg16, stg32 = ip_bufs
            wiv = win.ap().rearrange("(kc p) c -> p kc c", p=128)
            pc = [0]
            ec = [0]
            for bi, (col0, kind, dst, off, scale, is32) in enumerate(blocks):
                s = bi % 2
                bld.dma('pool', lambda e, s=s, col0=col0: e.dma_start(out=wib[s][:, :, :], in_=wiv[:, :, col0:col0 + 512]),
                        f"wib{s}", [], [('wib', s)])
                if hook is not None and hook[0] == bi:
                    hook[1]()
                if kind == 'fm':
                    for cc in range(4):
                        for tt in range(NO // 512):
                            pb = pc[0] % 7
                            pc[0] += 1
                            for kc in range(8):
                                bld.op('pe', lambda e, pb=pb, kc=kc, cc=cc, tt=tt, s=s: e.matmul(
                                    psf[pb][:, :], lhsT=wib[s][:, kc, cc * 128:(cc + 1) * 128],
                                    rhs=hT[:, kc, tt * 512:(tt + 1) * 512], start=(kc == 0), stop=(kc == 7)),
                                    [('wib', s), ('hT', tt)], [('psf', pb)])
                            k = ec[0] % 2
                            ec[0] += 1
                            st, sk = (stg32[k], ('stg32', k)) if is32 else (stg16[k], ('stg16', k))
                            if k == 0:
                                bld.op('act', lambda e, st=st, pb=pb, scale=scale: e.activation(
                                    out=st[:, :], in_=psf[pb][:, :], func=AF.Copy, scale=scale), [('psf', pb)], [sk])
                            else:
                                bld.op('dve', lambda e, st=st, pb=pb, scale=scale: e.tensor_scalar(
                                    out=st[:, :], in0=psf[pb][:, :], scalar1=scale, scalar2=None, op0=ALU.mult),
                                    [('psf', pb)], [sk])
                            r0 = off + cc * 128
                            if isinstance(dst, list):
                                dt_, c0 = dst[(tt * 512) // 1024], (tt * 512) % 1024
                            else:
                                dt_, c0 = dst, tt * 512
                            bld.dma('sp', lambda e, st=st, dt_=dt_, r0=r0, c0=c0: e.dma_start(
                                out=dt_[r0:r0 + 128, c0:c0 + 512], in_=st[:, :]),
                                f"{sk[0]}_{sk[1]}", [sk], [])
                else:
                    for tsub in range(NTILE):
                        pb = pc[0] % 7
                        pc[0] += 1
                        for kc in range(8):
                            bld.op('pe', lambda e, pb=pb, kc=kc, tsub=tsub, s=s: e.matmul(
                                psf[pb][:, :], lhsT=hT[:, kc, tsub * 128:(tsub + 1) * 128],
                                rhs=wib[s][:, kc, :], start=(kc == 0), stop=(kc == 7)),
                                [('wib', s), ('hT', tsub // 4)], [('psf', pb)])
                        k = ec[0] % 2
                        ec[0] += 1
                        st, sk = stg16[k], ('stg16', k)
                        if k == 0:
                            bld.op('act', lambda e, st=st, pb=pb: e.activation(
                                out=st[:, :], in_=psf[pb][:, :], func=AF.Copy), [('psf', pb)], [sk])
                        else:
                            bld.op('dve', lambda e, st=st, pb=pb: e.tensor_copy(out=st[:, :], in_=psf[pb][:, :]),
                                   [('psf', pb)], [sk])
                        bld.dma('sp', lambda e, st=st, dst=dst, tsub=tsub: e.dma_start(
                            out=dst[tsub * 128:(tsub + 1) * 128, :], in_=st[:, :]), f"{sk[0]}_{sk[1]}", [sk], [])

        def alloc_tok(es):
            hT = sbt(es, "hT", [128, 8, NO], BF16)
            xh = [sbt(es, f"xh{k}", [128, D], BF16) for k in range(2)]
            sqj = sbt(es, "sqj", [128, D], F32)
            return (hT, xh, sqj)

        def alloc_ffn(es):
            wgb = [sbt(es, f"wgb{k}", [128, 8, 256], BF16) for k in range(2)]
            wub = [sbt(es, f"wub{k}", [128, 8, 256], BF16) for k in range(2)]
            wdb = [sbt(es, f"wdb{k}", [128, 2, D], BF16) for k in range(2)]
            sgb = [sbt(es, f"sgb{k}", [128, 512], F32) for k in range(2)]
            aTb = [sbt(es, f"aTb{k}", [128, 2, 512], BF16) for k in range(2)]
            return (wgb, wub, wdb, sgb, aTb)

        BLK = [(2048, 'fm', SCt[0], 0, 1.0, True), (1536, 'fm', SCt[1], 0, 1.0, True), (3072, 'fm', SCt[2], 0, 1.0, True),
               (2560, 'tm', SBi, 0, 1.0, False),
               (512, 'fm', SAk, 0, 1.0, False), (0, 'fm', SAq, 0, 0.125, False), (1024, 'tm', SBv, 0, 1.0, False)]

        def allgather(name, src, dst):
            bld._issue('pool', ('d', 'cc_' + name), 0, lambda e: e.collective_compute(
                "AllGather", ALU.bypass, replica_groups=PAIRS, ins=[src.ap().opt()], outs=[dst.ap().opt()]), [], [])

        def gather(name, out_ap, src_view, col, npart, reads_extra, writes):
            bld.dma('pool', lambda e: e.indirect_dma_start(
                out=out_ap, out_offset=None, in_=src_view,
                in_offset=bass.IndirectOffsetOnAxis(ap=idxt[0:npart, col:col + 1], axis=0)), name, ['idxt'] + reads_extra, writes)

        with ExitStack() as es1:
            psf, pst = alloc_psum(es1)
            tokb = alloc_tok(es1)
            ffnb = alloc_ffn(es1)
            wib = [sbt(es1, f"wib{k}", [128, 8, 512], BF16) for k in range(2)]
            stg16 = [sbt(es1, f"stg16_{k}", [128, 512], BF16) for k in range(2)]
            stg32 = [sbt(es1, f"stg32_{k}", [128, 512], F32) for k in range(2)]
            ipb = (wib, stg16, stg32)
            norm_T(tokb, 0, 0)
            ffn_groups(tokb, ffnb, w1g, w1u, w1d)
            norm_T(tokb, None, 1)
            def early_gathers():
                bld.wait_all('pool')
                for k in range(3):
                    for t in range(SUB):
                        allgather(f'c{k}{t}', SCt[k][t], GCt[k][t])
                allgather('bi', SBi, GBi)

            in_proj(tokb, ipb, BLK, hook=(5, early_gathers))
            if dbg:
                for i in range(NTILE):
                    bld.dma('sp', lambda e, i=i: e.dma_start(out=d_x1[i * 128:(i + 1) * 128, :], in_=res[:, i, :]),
                            f"res{i}", [('res', i)], [])
            bld.barrier()
            if stop == 0:
                bld.flush()
            finish(0)
            allgather('ak', SAk, GAk)
            allgather('aq', SAq, GAq)
            allgather('bv', SBv, GBv)
            bld.barrier(exclude={('d', 'cc_ak'), ('d', 'cc_aq'), ('d', 'cc_bv')})
            bld.flush()
            finish(1)

        GBi4 = GBi.ap().rearrange("t (q c) -> (t q) c", c=128)
        GBv2 = GBv.ap().rearrange("t (q c) -> (t q) c", c=256)
        G2Av = G2A.ap().rearrange("r (h t) -> (r h) t", h=2)
        G2Bv = G2B.ap().rearrange("r (h t) -> (r h) t", h=2)

        with ExitStack() as es2:
            ohg = sbt(es2, "ohg", [128, 2, NT], BF16)
            osb = sbt(es2, "osb", [64, 4, NT], BF16)
            sqb2 = [sbt(es2, f"sqb{k}", [128, 512], F32) for k in range(2)]
            rsb2 = [sbt(es2, f"rsb{k}", [128, 512], F32) for k in range(2)]
            tmpb2 = [sbt(es2, f"tmpb{k}", [128, 512], F32) for k in range(2)]
            sqb, rsb, tmpb = sqb2[0], rsb2[0], tmpb2[0]

            with ExitStack() as es:
                psf, pst = alloc_psum(es)
                hA = sbt(es, "hA", [128, PZ], F32)
                hB = sbt(es, "hB", [128, PZ], F32)
                hC = sbt(es, "hC", [128, PZ], F32)
                Kt16 = sbt(es, "Kt16", [128, PZ], BF16)
                KhT = sbt(es, "KhT", [128, PZ], BF16)
                KhS = sbt(es, "KhS", [64, 16, 128], BF16)
                Vcs = [sbt(es, f"Vc{k}", [64, 16, 128], BF16) for k in range(2)]
                qin = sbt(es, "qin", [128, PZ], F32)
                Q1 = sbt(es, "Q1", [128, PZ], F32)
                Qt16 = sbt(es, "Qt16", [128, PZ], BF16)
                gin = sbt(es, "gin", [128, PZ], F32)
                G1 = sbt(es, "G1", [128, PZ], F32)
                dec = sbt(es, "dec", [128, 16], F32)
                S = sbt(es, "S", [128, 128], F32)
                S16 = sbt(es, "S16", [128, 16, 128], BF16)
                sc16 = [sbt(es, f"sc16_{k}", [64, 512], BF16) for k in range(2)]
                for hd in range(2):
                    bld.op('dve', lambda e: e.memset(S[:, :], 0.0), [], ['S'])
                    for p in range(NT // PZ):
                        t0 = p * PZ
                        own = True
                        o0 = t0
                        cb = LX['C'] + hd * LX['NPC'] + p
                        th = (t0 % NO) // 1024
                        vk = (hd * (NT // PZ) + p) % 2
                        Vc = Vcs[vk]
                        for c in range(16):
                            gather(f'Vc{vk}', Vc[:, c, :], GBi4, LX['V'] + hd * LX['NCH'] + p * 16 + c, 64, [], [('Vc', vk)])
                        gather('hA', hA[:, :], GCt[0][th][:, :], cb, 128, [], ['hA'])
                        gather('qin', qin[:, :], GCt[1][th][:, :], cb, 128, [], ['qin'])
                        gather('gin', gin[:, :], GCt[2][th][:, :], cb, 128, [], ['gin'])
                        bld.op('act', lambda e: e.activation(out=hB[:, :], in_=hA[:, :], func=AF.Exp, scale=-1.0), ['hA'], ['hB'])
                        bld.op('act', lambda e: e.activation(out=hB[:, :], in_=hB[:, :], func=AF.Ln, bias=1.0), ['hB'], ['hB'])
                        bld.op('act', lambda e: e.activation(out=hB[:, :], in_=hB[:, :], func=AF.Exp, scale=-1.0), ['hB'], ['hB'])
                        bld.op('dve', lambda e, hd=hd: e.tensor_scalar(
                            out=hB[:, :], in0=hB[:, :], scalar1=omlt[:, hd:hd + 1], scalar2=lbt[:, hd:hd + 1],
                            op0=ALU.mult, op1=ALU.add), ['hB', 'omlt', 'lbt'], ['hB'])
                        bld.op('act', lambda e: e.activation(out=hA[:, :], in_=hB[:, :], func=AF.Ln), ['hB'], ['hA'])
                        bld.op('pool', lambda e: e.tensor_scalar(out=hB[:, :], in0=hB[:, :], scalar1=-1.0, scalar2=1.0,
                                                                 op0=ALU.mult, op1=ALU.add), ['hB', 'hA'], ['hB'])
                        bld.op('dve', lambda e: e.tensor_tensor_scan(
                            out=hC[:, :], data0=scanmask[:, 0:PZ], data1=hA[:, :], initial=0.0, op0=ALU.mult, op1=ALU.add),
                            ['scanmask', 'hA'], ['hC'])
                        bld.op('act', lambda e: e.activation(out=hA[:, :], in_=hC[:, :], func=AF.Exp), ['hC'], ['hA'])
                        bld.op('act', lambda e: e.activation(out=hC[:, :], in_=hC[:, :], func=AF.Exp, scale=-1.0), ['hC', 'hA'], ['hC'])
                        bld.op('dve', lambda e: e.tensor_tensor(out=hC[:, :], in0=hB[:, :], in1=hC[:, :], op=ALU.mult),
                               ['hB', 'hC'], ['hC'])
                        bld.op('dve', lambda e: e.tensor_copy(
                            out=dec[:, :], in_=hA[:, :].rearrange("p (c s) -> p c s", s=64)[:, :, 63]), ['hA'], ['dec'])
                        bld.op('act', lambda e: e.activation(out=Kt16[:, :], in_=hC[:, :], func=AF.Copy), ['hC'], ['Kt16'])
                        bld.op('dve', lambda e: e.tensor_tensor(
                            out=KhT[:, :].rearrange("p (c s) -> p c s", s=64), in0=hC[:, :].rearrange("p (c s) -> p c s", s=64),
                            in1=dec[:, 0:16].unsqueeze(2).to_broadcast([128, 16, 64]), op=ALU.mult), ['hC', 'dec'], ['KhT'])
                        for r in range(2):
                            for k in range(8):
                                c = r * 8 + k
                                bld.op('pe', lambda e, c=c, k=k: e.transpose(
                                    out=pst[0:64, k * 128:(k + 1) * 128], in_=KhT[:, c * 64:(c + 1) * 64], identity=ident[:, :]),
                                    ['KhT', 'ident'], ['pst'])
                            bld.op('act', lambda e, r=r: e.activation(
                                out=KhS[:, r * 8:(r + 1) * 8, :], in_=pst[0:64, :].rearrange("p (k d) -> p k d", k=8), func=AF.Copy),
                                ['pst'], ['KhS'])
                        if own:
                            bld.op('act', lambda e: e.activation(out=Q1[:, :], in_=qin[:, :], func=AF.Exp, scale=-1.0), ['qin'], ['Q1'])
                            bld.op('act', lambda e: e.activation(out=Q1[:, :], in_=Q1[:, :], func=AF.Ln, bias=1.0), ['Q1'], ['Q1'])
                            bld.op('act', lambda e: e.activation(out=Q1[:, :], in_=Q1[:, :], func=AF.Exp, scale=-1.0), ['Q1'], ['Q1'])
                            bld.op('pool', lambda e: e.tensor_tensor(out=Q1[:, :], in0=Q1[:, :], in1=qin[:, :], op=ALU.mult),
                                   ['Q1', 'qin'], ['Q1'])
                            bld.op('dve', lambda e: e.tensor_tensor(out=Qt16[:, :], in0=Q1[:, :], in1=hA[:, :], op=ALU.mult),
                                   ['Q1', 'hA'], ['Qt16'])
                            bld.op('act', lambda e: e.activation(out=G1[:, :], in_=gin[:, :], func=AF.Exp, scale=-1.0), ['gin'], ['G1'])
                            bld.op('act', lambda e: e.activation(out=G1[:, :], in_=G1[:, :], func=AF.Ln, bias=1.0), ['G1'], ['G1'])
                            bld.op('act', lambda e: e.activation(out=G1[:, :], in_=G1[:, :], func=AF.Exp, scale=-1.0), ['G1'], ['G1'])
                            bld.op('pool', lambda e: e.tensor_tensor(out=G1[:, :], in0=G1[:, :], in1=gin[:, :], op=ALU.mult),
                                   ['G1', 'gin'], ['G1'])
                        DB = (0, 1, 2, 6)
                        for c in range(16):
                            db = DB[c // 4]
                            bld.op('pe', lambda e, c=c, db=db, Vc=Vc: e.matmul(
                                psf[db][:, (c % 4) * 128:(c % 4 + 1) * 128], lhsT=KhS[:, c, :], rhs=Vc[:, c, :],
                                start=True, stop=True), ['KhS', ('Vc', vk)], [('psf', db)])
                        for c in range(16):
                            db = DB[c // 4]
                            if own:
                                bld.op('dve', lambda e, c=c: e.tensor_copy(out=S16[:, c, :], in_=S[:, :]), ['S'], ['S16'])
                            bld.op('dve', lambda e, c=c, db=db: e.scalar_tensor_tensor(
                                out=S[:, :], in0=S[:, :], scalar=dec[:, c:c + 1], in1=psf[db][:, (c % 4) * 128:(c % 4 + 1) * 128],
                                op0=ALU.mult, op1=ALU.add), ['S', 'dec', ('psf', db)], ['S'])
                        if own:
                            for grp in range(2):
                                for k in range(8):
                                    c = grp * 8 + k
                                    bld.op('pe', lambda e, c=c, k=k, grp=grp: e.matmul(
                                        psf[3 + grp][0:64, k * 64:(k + 1) * 64], lhsT=Kt16[:, c * 64:(c + 1) * 64],
                                        rhs=Qt16[:, c * 64:(c + 1) * 64], start=True, stop=True),
                                        ['Kt16', 'Qt16'], [('psf', 3 + grp)])
                            for grp in range(2):
                                bld.op('dve', lambda e, grp=grp: e.tensor_tensor(
                                    out=sc16[grp][:, :], in0=psf[3 + grp][0:64, :], in1=incl[0:64, :], op=ALU.mult),
                                    [('psf', 3 + grp), 'incl'], [('sc16', grp)])
                            for grp in range(2):
                                for k in range(8):
                                    c = grp * 8 + k
                                    bld.op('pe', lambda e, c=c, k=k, grp=grp, Vc=Vc: e.matmul(
                                        psf[grp][:, k * 64:(k + 1) * 64], lhsT=Vc[:, c, :], rhs=sc16[grp][:, k * 64:(k + 1) * 64],
                                        start=True, stop=False), [('Vc', vk), ('sc16', grp)], [('psf', grp)])
                                    bld.op('pe', lambda e, c=c, k=k, grp=grp: e.matmul(
                                        psf[grp][:, k * 64:(k + 1) * 64], lhsT=S16[:, c, :], rhs=Qt16[:, c * 64:(c + 1) * 64],
                                        start=False, stop=True), ['S16', 'Qt16'], [('psf', grp)])
                            MB = (2, 5)
                            for grp in range(2):
                                bld.op('act', lambda e, grp=grp: e.activation(out=sqb2[grp][:, :], in_=psf[grp][:, :], func=AF.Square),
                                       [('psf', grp)], [('sqb', grp)])
                            for grp in range(2):
                                bld.op('pe', lambda e, grp=grp: e.matmul(psf[MB[grp]][:, :], lhsT=ones128[:, :], rhs=sqb2[grp][:, :],
                                                                         start=True, stop=True),
                                       ['ones128', ('sqb', grp)], [('psf', MB[grp])])
                            for grp in range(2):
                                bld.op('act', lambda e, grp=grp: e.activation(out=rsb2[grp][:, :], in_=psf[MB[grp]][:, :], func=AF.Ln, bias=EPS),
                                       [('psf', MB[grp])], [('rsb', grp)])
                            for grp in range(2):
                                bld.op('act', lambda e, grp=grp: e.activation(out=rsb2[grp][:, :], in_=rsb2[grp][:, :], func=AF.Exp, scale=-0.5),
                                       [('rsb', grp)], [('rsb', grp)])
                            for grp in range(2):
                                bld.op('dve', lambda e, grp=grp: e.tensor_tensor(out=tmpb2[grp][:, :], in0=psf[grp][:, :], in1=rsb2[grp][:, :],
                                                                                 op=ALU.mult),
                                       [('psf', grp), ('rsb', grp)], [('tmpb', grp)])
                            for grp in range(2):
                                bld.op('dve', lambda e, hd=hd, o0=o0, grp=grp: e.scalar_tensor_tensor(
                                    out=ohg[:, hd, o0 + grp * 512: o0 + (grp + 1) * 512], in0=tmpb2[grp][:, :],
                                    scalar=smallt[:, 8 + hd: 9 + hd], in1=G1[:, grp * 512:(grp + 1) * 512],
                                    op0=ALU.mult, op1=ALU.mult), [('tmpb', grp), 'smallt', 'G1'], ['ohg'])
                bld.barrier()
                bld.flush()
                finish(2)

            with ExitStack() as es:
                uid[0] += 1
                psX = es.enter_context(nc.psum_tensor(f"psX_{uid[0]}", [128, 3, 1024], F32))
                psOT = es.enter_context(nc.psum_tensor(f"psOT_{uid[0]}", [128, 512], F32))
                psM = es.enter_context(nc.psum_tensor(f"psM_{uid[0]}", [128, 512], F32))
                KTh = [sbt(es, "KTh0", [64, NT], BF16)]
                QTh = [sbt(es, "QTh0", [64, NT], BF16)]
                Vatt = sbt(es, "Vatt", [128, NT // 128, 256], BF16)
                Eb = [sbt(es, f"Eb{k}", [128, 1024], F32) for k in range(2)]
                SPb = [sbt(es, f"SPb{k}", [128, 1024], BF16) for k in range(2)]
                A16 = sbt(es, "A16", [128, 512], BF16)
                Wb = [sbt(es, f"Wb{k}", [128, 1024], BF16) for k in range(2)]
                for nb in range(NT // 128):
                    gather('Vatt', Vatt[:, nb, :], GBv2, LX['B'] + nb, 128, [], ['Vatt'])

                def load_head(h):
                    sl = 0
                    for j in range(2):
                        gather(f"KTh{sl}", KTh[sl][:, j * NO:(j + 1) * NO], GAk[:, :], LX['A'] + h * 2 + j, 64, [], [('KTh', sl)])
                        gather(f"QTh{sl}", QTh[sl][:, j * NO:(j + 1) * NO], GAq[:, :], LX['A'] + h * 2 + j, 64, [], [('QTh', sl)])

                def filler(k):
                    for _ in range(k):
                        bld.op('pe', lambda e: e.matmul(psM[:, :], lhsT=ident[:, :], rhs=maskneg[:, 0:512],
                                                        start=True, stop=True), ['ident', 'maskneg'], ['psM'])

                def qk(p, kbs, n, xbase, sl, qs):
                    xk = (xbase + p) % 3
                    for half in range(2):
                        kb = kbs[2 * p + half]
                        j = kb - (n - 4)
                        xo = psX[:, xk, half * 512:(half + 1) * 512]
                        bld.op('pe', lambda e, kb=kb, j=j, xo=xo: e.matmul(
                            xo, lhsT=KTh[sl][:, kb * 128:(kb + 1) * 128], rhs=QTh[sl][:, qs],
                            start=True, stop=(j < 0)), [('KTh', sl), ('QTh', sl)], [('psX', xk)])
                        if j >= 0:
                            bld.op('pe', lambda e, j=j, xo=xo: e.matmul(
                                xo, lhsT=ident[:, :], rhs=maskneg[:, j * 512:(j + 1) * 512],
                                start=False, stop=True), ['ident', 'maskneg'], [('psX', xk)])

                def exp1(p, xbase):
                    xk = (xbase + p) % 3
                    ek = (xbase + p) % 2
                    bld.op('act', lambda e: e.activation(out=Eb[ek][:, :], in_=psX[:, xk, :], func=AF.Exp),
                           [('psX', xk)], [('Eb', ek)])

                def ln2(p, xbase):
                    ek = (xbase + p) % 2
                    spk = (xbase + p) % 2
                    bld.op('act', lambda e: e.activation(out=SPb[spk][:, :], in_=Eb[ek][:, :], func=AF.Ln, bias=1.0),
                           [('Eb', ek)], [('SPb', spk)])

                def cum(p, xbase):
                    xk = (xbase + p) % 3
                    spk = (xbase + p) % 2
                    for half in range(2):
                        first = (p == 0 and half == 0)
                        xo = psX[:, xk, half * 512:(half + 1) * 512]
                        spv = SPb[spk][:, half * 512:(half + 1) * 512]
                        bld.op('pe', lambda e, xo=xo, spv=spv, first=first: e.matmul(
                            xo, lhsT=negU[:, :], rhs=spv, start=False, stop=first, skip_group_check=True),
                            ['negU', ('SPb', spk)], [('psX', xk)])
                        if not first:
                            bld.op('pe', lambda e, xo=xo: e.matmul(
                                xo, lhsT=negOnes[:, :], rhs=A16[:, :], start=False, stop=True, skip_group_check=True),
                                ['negOnes', 'A16'], [('psX', xk)])
                        if first:
                            bld.op('dve', lambda e, spv=spv: e.tensor_copy(out=A16[:, :], in_=spv), [('SPb', spk)], ['A16'])
                        else:
                            bld.op('dve', lambda e, spv=spv: e.tensor_tensor(out=A16[:, :], in0=A16[:, :], in1=spv, op=ALU.add),
                                   ['A16', ('SPb', spk)], ['A16'])

                def expw(p, xbase):
                    xk = (xbase + p) % 3
                    wk = (xbase + p) % 2
                    bld.op('act', lambda e: e.activation(out=Wb[wk][:, :], in_=psX[:, xk, :], func=AF.Exp),
                           [('psX', xk)], [('Wb', wk)])

                def pv(p, kbs, n, xbase, sl, h):
                    wk = (xbase + p) % 2
                    for half in range(2):
                        kb = kbs[2 * p + half]
                        i = 2 * p + half
                        bld.op('pe', lambda e, kb=kb, i=i, half=half: e.matmul(
                            psOT[0:64, :], lhsT=Vatt[:, kb, h * 64:(h + 1) * 64], rhs=Wb[wk][:, half * 512:(half + 1) * 512],
                            start=(i == 0), stop=(i == n - 1)), ['Vatt', ('Wb', wk)], ['psOT'])

                xc = 0
                for h in range(4):
                    sl = 0
                    load_head(h)
                    for qt in range(NT // 512):
                        n = (qt * 512) // 128 + 4
                        kbs = list(range(n - 1, -1, -1))
                        P = n // 2
                        xbase = xc
                        xc += P
                        qs = slice(qt * 512, (qt + 1) * 512)
                        qk(0, kbs, n, xbase, sl, qs)
                        for p in range(P + 2):
                            if 1 <= p <= P:
                                cum(p - 1, xbase)
                            filler(NFILL)
                            if p < P:
                                exp1(p, xbase)
                            if p >= 2:
                                expw(p - 2, xbase)
                            if p + 1 < P:
                                qk(p + 1, kbs, n, xbase, sl, qs)
                            if p >= 2:
                                pv(p - 2, kbs, n, xbase, sl, h)
                            if p < P:
                                ln2(p, xbase)
                        bld.op('act', lambda e: e.activation(out=sqb[0:64, :], in_=psOT[0:64, :], func=AF.Square),
                               ['psOT'], ['sqb'])
                        bld.op('pe', lambda e: e.matmul(psM[0:64, :], lhsT=ones64[:, :], rhs=sqb[0:64, :], start=True, stop=True),
                               ['ones64', 'sqb'], ['psM'])
                        bld.op('act', lambda e: e.activation(out=rsb[0:64, :], in_=psM[0:64, :], func=AF.Ln, bias=EPS),
                               ['psM'], ['rsb'])
                        bld.op('act', lambda e: e.activation(out=rsb[0:64, :], in_=rsb[0:64, :], func=AF.Exp, scale=-0.5),
                               ['rsb'], ['rsb'])
                        bld.op('dve', lambda e: e.tensor_tensor(out=tmpb[0:64, :], in0=psOT[0:64, :], in1=rsb[0:64, :], op=ALU.mult),
                               ['psOT', 'rsb'], ['tmpb'])
                        bld.op('dve', lambda e, h=h, qs=qs: e.tensor_scalar(out=osb[:, h, qs], in0=tmpb[0:64, :], scalar1=smallt[0:64, h:h + 1],
                                                                          scalar2=None, op0=ALU.mult), ['tmpb', 'smallt'], ['osb'])
                bld.barrier()
                bld.flush()
                finish(3)

            for h in range(4):
                bld.dma('sp', lambda e, h=h: e.dma_start(out=S2A[h * 64:(h + 1) * 64, :], in_=osb[:, h, :]), 'x2a', ['osb'], [])
                if dbg:
                    bld.dma('sp', lambda e, h=h: e.dma_start(out=d_osb[h * 64:(h + 1) * 64, :], in_=osb[:, h, :]), 'dbg2', ['osb'], [])
            for hd in range(2):
                bld.dma('sp', lambda e, hd=hd: e.dma_start(out=S2B[hd * 128:(hd + 1) * 128, :], in_=ohg[:, hd, :]), 'x2b', ['ohg'], [])
                if dbg:
                    bld.dma('sp', lambda e, hd=hd: e.dma_start(out=d_ohg[hd * 128:(hd + 1) * 128, :], in_=ohg[:, hd, :]), 'dbg1', ['ohg'], [])
            bld.barrier()
            allgather('d', S2A, G2A)
            allgather('e', S2B, G2B)
            bld.barrier()
            bld.flush()
            finish(4)

        with ExitStack() as es:
            psf, pst = alloc_psum(es)
            wosb = sbt(es, "wosb", [64, 8, D], BF16)
            wohg = sbt(es, "wohg", [128, 4, D], BF16)
            msb = sbt(es, "msb", [64, 8, NO], BF16)
            mhg = sbt(es, "mhg", [128, 4, NO], BF16)
            bld.dma('pool', lambda e: e.dma_start(out=wosb[:, :, :], in_=wout.ap()[0:512, :].rearrange("(h d) m -> d h m", d=64)),
                    'wosb', [], ['wosb'])
            bld.dma('pool', lambda e: e.dma_start(out=wohg[:, :, :], in_=wout.ap()[512:1024, :].rearrange("(h p) m -> p h m", p=128)),
                    'wohg', [], ['wohg'])
            for h in range(8):
                gather('msb', msb[:, h, :], G2Av, LX['E'] + h, 64, [], ['msb'])
            for hd in range(4):
                gather('mhg', mhg[:, hd, :], G2Bv, LX['E'] + 8 + hd, 128, [], ['mhg'])
            yc = 0
            for ti in range(NTILE):
                ts = slice(ti * 128, (ti + 1) * 128)
                for half in range(2):
                    hs = slice(half * 512, (half + 1) * 512)
                    yb = yc % 4
                    yc += 1
                    for h in range(8):
                        bld.op('pe', lambda e, h=h, yb=yb, ts=ts, hs=hs: e.matmul(
                            psf[yb][:, :], lhsT=msb[:, h, ts], rhs=wosb[:, h, hs], start=(h == 0), stop=False),
                            ['msb', 'wosb'], [('psf', yb)])
                    for hd in range(4):
                        bld.op('pe', lambda e, hd=hd, yb=yb, ts=ts, hs=hs: e.matmul(
                            psf[yb][:, :], lhsT=mhg[:, hd, ts], rhs=wohg[:, hd, hs], start=False, stop=(hd == 3)),
                            ['mhg', 'wohg'], [('psf', yb)])
                    bld.op('dve', lambda e, yb=yb, ti=ti, hs=hs: e.tensor_tensor(
                        out=res[:, ti, hs], in0=psf[yb][:, :], in1=res[:, ti, hs], op=ALU.add),
                        [('psf', yb), ('res', ti)], [('res', ti)])
            bld.barrier()
            bld.flush()

        with ExitStack() as es3:
            psf, pst = alloc_psum(es3)
            tokb = alloc_tok(es3)
            ffnb = alloc_ffn(es3)
            ost = [sbt(es3, f"ost{k}", [128, D], F32) for k in range(2)]
            norm_T(tokb, None, 2)
            ffn_groups(tokb, ffnb, w2g, w2u, w2d)
            sqj = tokb[2]
            bld.dma('sp', lambda e: e.dma_start(out=gt[:, :], in_=gains[3, :, :]), 'gt', [], ['gt'])
            for i in range(NTILE):
                bld.op('act', lambda e, i=i: e.activation(out=sqj[:, :], in_=res[:, i, :], func=AF.Square),
                       [('res', i)], ['sqj'])
                bld.op('dve', lambda e, i=i: e.reduce_sum(out=ss[:, i:i + 1], in_=sqj[:, :], axis=AX.X), ['sqj'], ['ss'])
            bld.op('act', lambda e: e.activation(out=rstd[:, 0:NTILE], in_=ss[:, 0:NTILE], func=AF.Ln, scale=1.0 / D, bias=EPS),
                   ['ss'], ['rstd'])
            bld.op('act', lambda e: e.activation(out=rstd[:, 0:NTILE], in_=rstd[:, 0:NTILE], func=AF.Exp, scale=-0.5),
                   ['rstd'], ['rstd'])
            for i in range(NTILE):
                k = i % 2
                bld.op('dve', lambda e, i=i, k=k: e.scalar_tensor_tensor(
                    out=ost[k][:, :], in0=res[:, i, :], scalar=rstd[:, i:i + 1], in1=gt[:, :], op0=ALU.mult, op1=ALU.mult),
                    [('res', i), 'rstd', 'gt'], [('ost', k)])
                bld.dma('sp', lambda e, i=i, k=k: e.dma_start(out=out[i * 128:(i + 1) * 128, :], in_=ost[k][:, :]),
                        f"ost{k}", [('ost', k)], [])
            bld.barrier()
            bld.flush()
    except _Stop:
        pass
    return nc


def make_consts():
    c = np.zeros((128, CW), np.float32)
    c[:, 0:128] = np.eye(128, dtype=np.float32)
    j = np.arange(128)[:, None]
    s = np.arange(128)[None, :]
    c[:, 128:256] = np.where(j >= s, -1.0, 0.0)
    sp = np.arange(128)[:, None]
    t = np.arange(512)[None, :]
    for jj in range(4):
        m = (t < 128 * jj) | ((t < 128 * (jj + 1)) & (sp >= t - 128 * jj))
        c[:, 256 + jj * 512: 256 + (jj + 1) * 512] = np.where(m, NEG, 0.0)
    sm = np.ones(1024, np.float32)
    sm[::64] = 0.0
    c[:, 2304:3328] = sm[None, :]
    s64 = np.arange(64)[:, None]
    t64 = np.arange(64)[None, :]
    inc = np.where(s64 <= t64, 1.0, 0.0).astype(np.float32)
    c[0:64, 3328:3840] = np.tile(inc, (1, 8))
    return c


def make_core_inputs(xo, p, r):
    NO = xo.shape[0]
    small = np.zeros((128, 20), np.float32)
    small[0:64, 0:4] = p["sb_out_norm"].reshape(8, 64).T[:, 4 * r:4 * r + 4]
    small[:, 8:10] = p["hg_out_norm"].reshape(4, 128).T[:, 2 * r:2 * r + 2]
    small[:, 12:14] = p["hg_lower_bound_logits"][0].reshape(4, 128).T[:, 2 * r:2 * r + 2]
    small[:, 16:18] = p["hg_lower_bound_logits"][1].reshape(4, 128).T[:, 2 * r:2 * r + 2]
    gains = np.stack([np.broadcast_to(p[k][None, :], (128, D)) for k in ("ffn1_norm", "mix_norm", "ffn2_norm", "final_norm")])
    return {
        "xin": np.ascontiguousarray(xo, dtype=np.float32),
        "idxd": make_idx(r, NO),
        "gains": np.ascontiguousarray(gains, dtype=np.float32),
        "cst": make_consts(),
        "small": small,
        "w1g": p["ffn1_w_gate"], "w1u": p["ffn1_w_up"], "w1d": p["ffn1_w_down"],
        "w2g": p["ffn2_w_gate"], "w2u": p["ffn2_w_up"], "w2d": p["ffn2_w_down"],
        "win": p["w_in"], "wout": p["w_out"],
    }


def squeeze_params(inputs):
    p = {}
    for k, v in inputs.items():
        if k == "x":
            continue
        v = np.asarray(v, dtype=np.float32)
        if k in ("final_norm", "hg_lower_bound_logits"):
            p[k] = np.ascontiguousarray(v)
        else:
            p[k] = np.ascontiguousarray(v[0])
    return p


def kernel(**inputs):
    x = np.asarray(inputs["x"], dtype=np.float32)
    B, T, _ = x.shape
    H = T // 2
    p = squeeze_params(inputs)
    nc = build(H)
    in_maps = []
    for c in range(8):
        b, hh = c // 2, c % 2
        in_maps.append(make_core_inputs(x[b, hh * H:(hh + 1) * H], p, hh))
    res = run_bass_kernel_spmd(nc, in_maps, core_ids=list(range(8)))
    out = np.zeros((B, T, D), np.float32)
    for c in range(8):
        b, hh = c // 2, c % 2
        out[b, hh * H:(hh + 1) * H] = np.asarray(res.results[c]["out"], dtype=np.float32)
    return out
```

```python
import numpy as np
from contextlib import ExitStack
import concourse.bass as bass
import concourse.mybir as mybir
from concourse.bass_utils import run_bass_kernel_spmd

F32 = mybir.dt.float32
BF16 = mybir.dt.bfloat16
AF = mybir.ActivationFunctionType
ALU = mybir.AluOpType
AX = mybir.AxisListType

D = 1024
DFF = 2816
NGRP = 11
EPS = 1e-6
NEG = -30000.0
NFILL = 1
CW = 128 + 128 + 2048 + 1024 + 512
ENGS = ('pe', 'act', 'dve', 'pool', 'sp')
EMAP = {'pe': 'tensor', 'act': 'scalar', 'dve': 'vector', 'pool': 'gpsimd', 'sp': 'sync'}


class Builder:
    def __init__(self, nc, es):
        self.nc = nc
        self.es = es
        self.sems = {}
        self.cnt = {}
        self.streams = {e: [] for e in ENGS}
        self.known = {e: {} for e in ENGS}
        self.state = {}
        self.snap = {}

    def _sem(self, key):
        if key not in self.sems:
            name = "s_" + (key if isinstance(key, str) else "d_" + str(key[1]))
            self.sems[key] = self.es.enter_context(self.nc.semaphore(name))
            self.cnt[key] = 0
        return self.sems[key]

    def _issue(self, eng, semkey, inc, fn, reads, writes):
        deps = {}

        def add(t):
            if t is not None and deps.get(t[0], 0) < t[1]:
                deps[t[0]] = t[1]
        for r in reads:
            st = self.state.get(r)
            if st:
                add(st[0])
        for w in writes:
            st = self.state.get(w)
            if st:
                add(st[0])
                for k, v in st[1].items():
                    add((k, v))
        kn = self.known[eng]
        waits = []
        for k, v in deps.items():
            if k == 'pe' and eng == 'pe':
                continue
            if kn.get(k, 0) >= v:
                continue
            waits.append((k, v))
        for k, v in waits:
            if kn.get(k, 0) < v:
                kn[k] = v
            sn = self.snap.get((k, v))
            if sn:
                for kk, vv in sn.items():
                    if kn.get(kk, 0) < vv:
                        kn[kk] = vv
        self._sem(semkey)
        self.cnt[semkey] += (inc or 1)
        val = self.cnt[semkey]
        self.snap[(semkey, val)] = dict(kn)
        self.streams[eng].append((waits, fn, semkey, inc))
        tok = (semkey, val)
        for w in writes:
            self.state[w] = [tok, {}]
        for r in reads:
            st = self.state.setdefault(r, [None, {}])
            st[1][semkey] = val
        return tok

    def op(self, eng, fn, reads, writes):
        return self._issue(eng, eng, 1, fn, reads, writes)

    def dma(self, queue, fn, name, reads, writes):
        return self._issue(queue, ('d', name), 16, fn, reads, writes)

    def barrier(self, exclude=()):
        for e in ENGS:
            kn = self.known[e]
            waits = []
            for k, c in self.cnt.items():
                if k in exclude:
                    continue
                if c > 0 and kn.get(k, 0) < c:
                    waits.append((k, c))
                    kn[k] = c
            if waits:
                self.streams[e].append((waits, None, None, 0))
        self.state = {}
        self.snap = {}

    def wait_all(self, eng):
        kn = self.known[eng]
        waits = []
        for k, c in self.cnt.items():
            if c > 0 and kn.get(k, 0) < c:
                waits.append((k, c))
                kn[k] = c
        if waits:
            self.streams[eng].append((waits, None, None, 0))

    def flush(self):
        streams = self.streams
        self.streams = {e: [] for e in ENGS}
        sems = self.sems
        with self.nc.Block() as block:
            for e in ENGS:
                lst = streams[e]
                if not lst:
                    continue

                def body(eng, lst=lst):
                    for waits, fn, semkey, inc in lst:
                        for k, v in waits:
                            eng.wait_ge(sems[k], v)
                        if fn is not None:
                            if inc:
                                fn(eng).then_inc(sems[semkey], inc)
                            else:
                                fn(eng).then_inc(sems[semkey])
                getattr(block, EMAP[e])(body)


def idx_layout(NO):
    NT = 2 * NO
    L = {}
    L['A'] = 0
    L['B'] = 8
    L['C'] = L['B'] + NT // 128
    L['NPC'] = NT // 1024
    L['V'] = L['C'] + 2 * L['NPC']
    L['NCH'] = NT // 64
    L['E'] = L['V'] + 2 * L['NCH']
    L['N'] = L['E'] + 12
    return L


def make_idx(r, NO):
    L = idx_layout(NO)
    NT = 2 * NO
    p = np.arange(128)
    ix = np.zeros((128, L['N']), np.int64)
    for hl in range(4):
        for j in range(2):
            ix[:64, L['A'] + hl * 2 + j] = j * 512 + (4 * r + hl) * 64 + p[:64]
    for n in range(NT // 128):
        ix[:, L['B'] + n] = 2 * (n * 128 + p) + r
    for hgl in range(2):
        for pc in range(L['NPC']):
            j = (pc * 1024) // NO
            ix[:, L['C'] + hgl * L['NPC'] + pc] = j * 512 + (2 * r + hgl) * 128 + p
    for hgl in range(2):
        for cg in range(L['NCH']):
            ix[:64, L['V'] + hgl * L['NCH'] + cg] = 4 * (cg * 64 + p[:64]) + 2 * r + hgl
    for h in range(8):
        ix[:64, L['E'] + h] = 2 * (h * 64 + p[:64]) + r
    for hd in range(4):
        ix[:, L['E'] + 8 + hd] = 2 * (hd * 128 + p) + r
    return ix.astype(np.int32)


class _Stop(Exception):
    pass


def build(NO, dbg=False, stop=99):
    NP = 0
    NT = 2 * NO
    PZ = 1024
    assert NO % PZ == 0
    LX = idx_layout(NO)
    nc = bass.Bass("TRN2", target_bir_lowering=False)
    xin = nc.dram_tensor("xin", [NO, D], F32, kind="ExternalInput")
    idxd = nc.dram_tensor("idxd", [128, LX['N']], mybir.dt.int32, kind="ExternalInput")
    gains = nc.dram_tensor("gains", [4, 128, D], F32, kind="ExternalInput")
    cst = nc.dram_tensor("cst", [128, CW], F32, kind="ExternalInput")
    small = nc.dram_tensor("small", [128, 20], F32, kind="ExternalInput")
    w1g = nc.dram_tensor("w1g", [D, DFF], F32, kind="ExternalInput")
    w1u = nc.dram_tensor("w1u", [D, DFF], F32, kind="ExternalInput")
    w1d = nc.dram_tensor("w1d", [DFF, D], F32, kind="ExternalInput")
    w2g = nc.dram_tensor("w2g", [D, DFF], F32, kind="ExternalInput")
    w2u = nc.dram_tensor("w2u", [D, DFF], F32, kind="ExternalInput")
    w2d = nc.dram_tensor("w2d", [DFF, D], F32, kind="ExternalInput")
    win = nc.dram_tensor("win", [D, 3584], F32, kind="ExternalInput")
    wout = nc.dram_tensor("wout", [D, D], F32, kind="ExternalInput")
    out = nc.dram_tensor("out", [NO, D], F32, kind="ExternalOutput")
    SUB = NO // 1024
    SAk = nc.dram_tensor("SAk", [512, NO], BF16)
    SAq = nc.dram_tensor("SAq", [512, NO], BF16)
    GAk = nc.dram_tensor("GAk", [1024, NO], BF16)
    GAq = nc.dram_tensor("GAq", [1024, NO], BF16)
    SBv = nc.dram_tensor("SBv", [NO, 512], BF16)
    SBi = nc.dram_tensor("SBi", [NO, 512], BF16)
    GBv = nc.dram_tensor("GBv", [2 * NO, 512], BF16)
    GBi = nc.dram_tensor("GBi", [2 * NO, 512], BF16)
    SCt = [[nc.dram_tensor(f"SC{k}_{t}", [512, 1024], F32) for t in range(SUB)] for k in range(3)]
    GCt = [[nc.dram_tensor(f"GC{k}_{t}", [1024, 1024], F32) for t in range(SUB)] for k in range(3)]
    S2A = nc.dram_tensor("S2A", [256, NT], BF16)
    S2B = nc.dram_tensor("S2B", [256, NT], BF16)
    G2A = nc.dram_tensor("G2A", [512, NT], BF16)
    G2B = nc.dram_tensor("G2B", [512, NT], BF16)
    PAIRS = [[0, 1], [2, 3], [4, 5], [6, 7]]
    if dbg:
        d_x1 = nc.dram_tensor("d_x1", [NO, D], F32, kind="ExternalOutput")
        d_ohg = nc.dram_tensor("d_ohg", [256, NT], BF16, kind="ExternalOutput")
        d_osb = nc.dram_tensor("d_osb", [256, NT], BF16, kind="ExternalOutput")

    try:
      with ExitStack() as es0:
        bld = Builder(nc, es0)

        def finish(k):
            if stop != k:
                return
            for i in range(NO // 128):
                bld.dma('sp', lambda e, i=i: e.dma_start(out=out[i * 128:(i + 1) * 128, :], in_=res[:, i, :]),
                        f"res{i}", [], [])
            bld.barrier()
            bld.flush()
            raise _Stop()

        uid = [0]

        def sbt(es, name, shape, dt):
            uid[0] += 1
            return es.enter_context(nc.sbuf_tensor(f"{name}_{uid[0]}", shape, dt))

        NTILE = NO // 128
        res = sbt(es0, "res", [128, NTILE, D], F32)
        gt = sbt(es0, "gt", [128, D], F32)
        ident = sbt(es0, "ident", [128, 128], BF16)
        negU = sbt(es0, "negU", [128, 128], BF16)
        negOnes = sbt(es0, "negOnes", [128, 128], BF16)
        maskneg = sbt(es0, "maskneg", [128, 2048], BF16)
        scanmask = sbt(es0, "scanmask", [128, 1024], BF16)
        incl = sbt(es0, "incl", [128, 512], BF16)
        ones128 = sbt(es0, "ones128", [128, 128], F32)
        ones64 = sbt(es0, "ones64", [64, 64], F32)
        smallt = sbt(es0, "smallt", [128, 20], F32)
        idxt = sbt(es0, "idxt", [128, LX['N']], mybir.dt.int32)
        lbt = sbt(es0, "lbt", [128, 4], F32)
        omlt = sbt(es0, "omlt", [128, 4], F32)
        ss = sbt(es0, "ss", [128, 16], F32)
        rstd = sbt(es0, "rstd", [128, 16], F32)
        def alloc_psum(es):
            uid[0] += 1
            u = uid[0]
            f = [es.enter_context(nc.psum_tensor(f"psf{i}_{u}", [128, 512], F32)) for i in range(7)]
            t = es.enter_context(nc.psum_tensor(f"pst_{u}", [128, 1024], BF16))
            return f, t

        bld.dma('pool', lambda e: e.dma_start(out=ident[:, :], in_=cst[:, 0:128]), 'c0', [], ['ident'])
        bld.dma('pool', lambda e: e.dma_start(out=negU[:, :], in_=cst[:, 128:256]), 'c1', [], ['negU'])
        bld.dma('pool', lambda e: e.dma_start(out=maskneg[:, :], in_=cst[:, 256:2304]), 'c2', [], ['maskneg'])
        bld.dma('pool', lambda e: e.dma_start(out=scanmask[:, :], in_=cst[:, 2304:3328]), 'c3', [], ['scanmask'])
        bld.dma('pool', lambda e: e.dma_start(out=incl[:, :], in_=cst[:, 3328:3840]), 'c4', [], ['incl'])
        bld.dma('sp', lambda e: e.dma_start(out=smallt[:, :], in_=small[:, :]), 'c5', [], ['smallt'])
        bld.dma('sp', lambda e: e.dma_start(out=idxt[:, :], in_=idxd[:, :]), 'c6', [], ['idxt'])
        bld.op('dve', lambda e: e.memset(negOnes[:, :], -1.0), [], ['negOnes'])
        bld.op('dve', lambda e: e.memset(ones128[:, :], 1.0 / 128.0), [], ['ones128'])
        bld.op('dve', lambda e: e.memset(ones64[:, :], 1.0 / 64.0), [], ['ones64'])
        bld.op('dve', lambda e: e.tensor_tensor(out=lbt[:, :], in0=smallt[:, 16:20], in1=smallt[:, 12:16], op=ALU.subtract),
               ['smallt'], ['lbt'])
        bld.op('act', lambda e: e.activation(out=lbt[:, :], in_=lbt[:, :], func=AF.Exp), ['lbt'], ['lbt'])
        bld.op('dve', lambda e: e.tensor_scalar(out=lbt[:, :], in0=lbt[:, :], scalar1=1.0, scalar2=None, op0=ALU.add),
               ['lbt'], ['lbt'])
        bld.op('dve', lambda e: e.reciprocal(out=lbt[:, :], in_=lbt[:, :]), ['lbt'], ['lbt'])
        bld.op('dve', lambda e: e.tensor_scalar(out=omlt[:, :], in0=lbt[:, :], scalar1=-1.0, scalar2=1.0,
                                                op0=ALU.mult, op1=ALU.add), ['lbt'], ['omlt'])

        def norm_T(es_bufs, src_row0, gidx):
            hT, xh, sqj = es_bufs
            bld.dma('sp', lambda e: e.dma_start(out=gt[:, :], in_=gains[gidx, :, :]), 'gt', [], ['gt'])
            for i in range(NTILE):
                if src_row0 is not None:
                    bld.dma('sp', lambda e, i=i: e.dma_start(out=res[:, i, :],
                                                              in_=xin[src_row0 + i * 128: src_row0 + (i + 1) * 128, :]),
                            f"res{i}", [], [('res', i)])
                bld.op('act', lambda e, i=i: e.activation(out=sqj[:, :], in_=res[:, i, :], func=AF.Square),
                       [('res', i)], ['sqj'])
                bld.op('dve', lambda e, i=i: e.reduce_sum(out=ss[:, i:i + 1], in_=sqj[:, :], axis=AX.X),
                       ['sqj'], ['ss'])
            bld.op('act', lambda e: e.activation(out=rstd[:, 0:NTILE], in_=ss[:, 0:NTILE], func=AF.Ln, scale=1.0 / D, bias=EPS),
                   ['ss'], ['rstd'])
            bld.op('act', lambda e: e.activation(out=rstd[:, 0:NTILE], in_=rstd[:, 0:NTILE], func=AF.Exp, scale=-0.5),
                   ['rstd'], ['rstd'])
            for i in range(NTILE):
                k = i % 2
                bld.op('dve', lambda e, i=i, k=k: e.scalar_tensor_tensor(
                    out=xh[k][:, :], in0=res[:, i, :], scalar=rstd[:, i:i + 1], in1=gt[:, :],
                    op0=ALU.mult, op1=ALU.mult), [('res', i), 'rstd', 'gt'], [('xh', k)])
                for kc in range(8):
                    bld.op('pe', lambda e, k=k, kc=kc: e.transpose(
                        out=pst[:, kc * 128:(kc + 1) * 128], in_=xh[k][:, kc * 128:(kc + 1) * 128], identity=ident[:, :]),
                        [('xh', k), 'ident'], ['pst'])
                bld.op('act', lambda e, i=i: e.activation(
                    out=hT[:, :, i * 128:(i + 1) * 128], in_=pst[:, :].rearrange("p (k t) -> p k t", k=8), func=AF.Copy),
                    ['pst'], [('hT', i // 4)])

        def ffn_groups(es_bufs, ffn_bufs, wg, wu, wd):
            hT = es_bufs[0]
            wgb, wub, wdb, sgb, aTb = ffn_bufs
            wgv = wg.ap().rearrange("(kc p) f -> p kc f", p=128)
            wuv = wu.ap().rearrange("(kc p) f -> p kc f", p=128)
            ntt = NTILE // 4

            def load(gi):
                s = gi % 2
                f0 = gi * 256
                bld.dma('pool', lambda e: e.dma_start(out=wgb[s][:, :, :], in_=wgv[:, :, f0:f0 + 256]),
                        f"wgb{s}", [], [('wgb', s)])
                bld.dma('pool', lambda e: e.dma_start(out=wub[s][:, :, :], in_=wuv[:, :, f0:f0 + 256]),
                        f"wub{s}", [], [('wub', s)])
                bld.dma('pool', lambda e: e.dma_start(
                    out=wdb[s][:, :, :], in_=wd.ap()[f0:f0 + 256, :].rearrange("(c p) m -> p c m", p=128)),
                    f"wdb{s}", [], [('wdb', s)])

            def gu(gi, tt, ab, c):
                s = gi % 2
                for (wb, wk, pb) in ((wgb, 'wgb', c), (wub, 'wub', 2 + c)):
                    for kc in range(8):
                        bld.op('pe', lambda e, wb=wb, pb=pb, kc=kc: e.matmul(
                            psf[pb][:, :], lhsT=wb[s][:, kc, c * 128:(c + 1) * 128],
                            rhs=hT[:, kc, tt * 512:(tt + 1) * 512], start=(kc == 0), stop=(kc == 7)),
                            [(wk, s), ('hT', tt)], [('psf', pb)])
                bld.op('act', lambda e: e.activation(out=sgb[c][:, :], in_=psf[c][:, :], func=AF.Silu),
                       [('psf', c)], [('sg', c)])
                bld.op('dve', lambda e: e.tensor_tensor(out=aTb[ab][:, c, :], in0=sgb[c][:, :],
                                                        in1=psf[2 + c][:, :], op=ALU.mult),
                       [('sg', c), ('psf', 2 + c)], [('aT', ab, c)])

            ycnt = [0]

            def down(gi, tt, ab, hf):
                s = gi % 2
                for sub in (2 * hf, 2 * hf + 1):
                    ti = tt * 4 + sub
                    for half in range(2):
                        yb = 4 + ycnt[0] % 3
                        ycnt[0] += 1
                        for c in range(2):
                            bld.op('pe', lambda e, yb=yb, c=c, sub=sub, half=half: e.matmul(
                                psf[yb][:, :], lhsT=aTb[ab][:, c, sub * 128:(sub + 1) * 128],
                                rhs=wdb[s][:, c, half * 512:(half + 1) * 512], start=(c == 0), stop=(c == 1)),
                                [('aT', ab, c), ('wdb', s)], [('psf', yb)])
                        bld.op('dve', lambda e, yb=yb, ti=ti, half=half: e.scalar_tensor_tensor(
                            out=res[:, ti, half * 512:(half + 1) * 512], in0=psf[yb][:, :], scalar=0.5,
                            in1=res[:, ti, half * 512:(half + 1) * 512], op0=ALU.mult, op1=ALU.add),
                            [('psf', yb), ('res', ti)], [('res', ti)])

            steps = [(gi, tt) for gi in range(NGRP) for tt in range(ntt)]
            load(0)
            prev = None
            for idx, (gi, tt) in enumerate(steps):
                if tt == min(1, ntt - 1) and gi + 1 < NGRP:
                    load(gi + 1)
                for c in range(2):
                    gu(gi, tt, idx % 2, c)
                    if prev is not None:
                        down(*prev, c)
                prev = (gi, tt, idx % 2)
            down(*prev, 0)
            down(*prev, 1)

        def in_proj(es_bufs, ip_bufs, blocks, hook=None):
            hT = es_bufs[0]
            wib, stg16, stg32 = ip_bufs
            wiv = win.ap().rearrange("(kc p) c -> p kc c", p=128)
            pc = [0]
            ec = [0]
            for bi, (col0, kind, dst, off, scale, is32) in enumerate(blocks):
                s = bi % 2
                bld.dma('pool', lambda e, s=s, col0=col0: e.dma_start(out=wib[s][:, :, :], in_=wiv[:, :, col0:col0 + 512]),
                        f"wib{s}", [], [('wib', s)])
                if hook is not None and hook[0] == bi:
                    hook[1]()
                if kind == 'fm':
                    for cc in range(4):
                        for tt in range(NO // 512):
                            pb = pc[0] % 7
                            pc[0] += 1
                            for kc in range(8):
                                bld.op('pe', lambda e, pb=pb, kc=kc, cc=cc, tt=tt, s=s: e.matmul(
                                    psf[pb][:, :], lhsT=wib[s][:, kc, cc * 128:(cc + 1) * 128],
                                    rhs=hT[:, kc, tt * 512:(tt + 1) * 512], start=(kc == 0), stop=(kc == 7)),
                                    [('wib', s), ('hT', tt)], [('psf', pb)])
                            k = ec[0] % 2
                            ec[0] += 1
                            st, sk = (stg32[k], ('stg32', k)) if is32 else (stg16[k], ('stg16', k))
                            if k == 0:
                                bld.op('act', lambda e, st=st, pb=pb, scale=scale: e.activation(
                                    out=st[:, :], in_=psf[pb][:, :], func=AF.Copy, scale=scale), [('psf', pb)], [sk])
                            else:
                                bld.op('dve', lambda e, st=st, pb=pb, scale=scale: e.tensor_scalar(
                                    out=st[:, :], in0=psf[pb][:, :], scalar1=scale, scalar2=None, op0=ALU.mult),
                                    [('psf', pb)], [sk])
                            r0 = off + cc * 128
                            if isinstance(dst, list):
                                dt_, c0 = dst[(tt * 512) // 1024], (tt * 512) % 1024
                            else:
                                dt_, c0 = dst, tt * 512
                            bld.dma('sp', lambda e, st=st, dt_=dt_, r0=r0, c0=c0: e.dma_start(
                                out=dt_[r0:r0 + 128, c0:c0 + 512], in_=st[:, :]),
                                f"{sk[0]}_{sk[1]}", [sk], [])
                else:
                    for tsub in range(NTILE):
                        pb = pc[0] % 7
                        pc[0] += 1
                        for kc in range(8):
                            bld.op('pe', lambda e, pb=pb, kc=kc, tsub=tsub, s=s: e.matmul(
                                psf[pb][:, :], lhsT=hT[:, kc, tsub * 128:(tsub + 1) * 128],
                                rhs=wib[s][:, kc, :], start=(kc == 0), stop=(kc == 7)),
                                [('wib', s), ('hT', tsub // 4)], [('psf', pb)])
                        k = ec[0] % 2
                        ec[0] += 1
                        st, sk = stg16[k], ('stg16', k)
                        if k == 0:
                            bld.op('act', lambda e, st=st, pb=pb: e.activation(
                                out=st[:, :], in_=psf[pb][:, :], func=AF.Copy), [('psf', pb)], [sk])
                        else:
                            bld.op('dve', lambda e, st=st, pb=pb: e.tensor_copy(out=st[:, :], in_=psf[pb][:, :]),
                                   [('psf', pb)], [sk])
                        bld.dma('sp', lambda e, st=st, dst=dst, tsub=tsub: e.dma_start(
                            out=dst[tsub * 128:(tsub + 1) * 128, :], in_=st[:, :]), f"{sk[0]}_{sk[1]}", [sk], [])

        def alloc_tok(es):
            hT = sbt(es, "hT", [128, 8, NO], BF16)
            xh = [sbt(es, f"xh{k}", [128, D], BF16) for k in range(2)]
            sqj = sbt(es, "sqj", [128, D], F32)
            return (hT, xh, sqj)

        def alloc_ffn(es):
            wgb = [sbt(es, f"wgb{k}", [128, 8, 256], BF16) for k in range(2)]
            wub = [sbt(es, f"wub{k}", [128, 8, 256], BF16) for k in range(2)]
            wdb = [sbt(es, f"wdb{k}", [128, 2, D], BF16) for k in range(2)]
            sgb = [sbt(es, f"sgb{k}", [128, 512], F32) for k in range(2)]
            aTb = [sbt(es, f"aTb{k}", [128, 2, 512], BF16) for k in range(2)]
            return (wgb, wub, wdb, sgb, aTb)

        BLK = [(2048, 'fm', SCt[0], 0, 1.0, True), (1536, 'fm', SCt[1], 0, 1.0, True), (3072, 'fm', SCt[2], 0, 1.0, True),
               (2560, 'tm', SBi, 0, 1.0, False),
               (512, 'fm', SAk, 0, 1.0, False), (0, 'fm', SAq, 0, 0.125, False), (1024, 'tm', SBv, 0, 1.0, False)]

        def allgather(name, src, dst):
            bld._issue('pool', ('d', 'cc_' + name), 0, lambda e: e.collective_compute(
                "AllGather", ALU.bypass, replica_groups=PAIRS, ins=[src.ap().opt()], outs=[dst.ap().opt()]), [], [])

        def gather(name, out_ap, src_view, col, npart, reads_extra, writes):
            bld.dma('pool', lambda e: e.indirect_dma_start(
                out=out_ap, out_offset=None, in_=src_view,
                in_offset=bass.IndirectOffsetOnAxis(ap=idxt[0:npart, col:col + 1], axis=0)), name, ['idxt'] + reads_extra, writes)

        with ExitStack() as es1:
            psf, pst = alloc_psum(es1)
            tokb = alloc_tok(es1)
            ffnb = alloc_ffn(es1)
            wib = [sbt(es1, f"wib{k}", [128, 8, 512], BF16) for k in range(2)]
            stg16 = [sbt(es1, f"stg16_{k}", [128, 512], BF16) for k in range(2)]
            stg32 = [sbt(es1, f"stg32_{k}", [128, 512], F32) for k in range(2)]
            ipb = (wib, stg16, stg32)
            norm_T(tokb, 0, 0)
            ffn_groups(tokb, ffnb, w1g, w1u, w1d)
            norm_T(tokb, None, 1)
            def early_gathers():
                bld.wait_all('pool')
                for k in range(3):
                    for t in range(SUB):
                        allgather(f'c{k}{t}', SCt[k][t], GCt[k][t])
                allgather('bi', SBi, GBi)

            in_proj(tokb, ipb, BLK, hook=(5, early_gathers))
            if dbg:
                for i in range(NTILE):
                    bld.dma('sp', lambda e, i=i: e.dma_start(out=d_x1[i * 128:(i + 1) * 128, :], in_=res[:, i, :]),
                            f"res{i}", [('res', i)], [])
            bld.barrier()
            if stop == 0:
                bld.flush()
            finish(0)
            allgather('ak', SAk, GAk)
            allgather('aq', SAq, GAq)
            allgather('bv', SBv, GBv)
            bld.barrier(exclude={('d', 'cc_ak'), ('d', 'cc_aq'), ('d', 'cc_bv')})
            bld.flush()
            finish(1)

        GBi4 = GBi.ap().rearrange("t (q c) -> (t q) c", c=128)
        GBv2 = GBv.ap().rearrange("t (q c) -> (t q) c", c=256)
        G2Av = G2A.ap().rearrange("r (h t) -> (r h) t", h=2)
        G2Bv = G2B.ap().rearrange("r (h t) -> (r h) t", h=2)

        with ExitStack() as es2:
            ohg = sbt(es2, "ohg", [128, 2, NT], BF16)
            osb = sbt(es2, "osb", [64, 4, NT], BF16)
            sqb2 = [sbt(es2, f"sqb{k}", [128, 512], F32) for k in range(2)]
            rsb2 = [sbt(es2, f"rsb{k}", [128, 512], F32) for k in range(2)]
            tmpb2 = [sbt(es2, f"tmpb{k}", [128, 512], F32) for k in range(2)]
            sqb, rsb, tmpb = sqb2[0], rsb2[0], tmpb2[0]

            with ExitStack() as es:
                psf, pst = alloc_psum(es)
                hA = sbt(es, "hA", [128, PZ], F32)
                hB = sbt(es, "hB", [128, PZ], F32)
                hC = sbt(es, "hC", [128, PZ], F32)
                Kt16 = sbt(es, "Kt16", [128, PZ], BF16)
                KhT = sbt(es, "KhT", [128, PZ], BF16)
                KhS = sbt(es, "KhS", [64, 16, 128], BF16)
                Vcs = [sbt(es, f"Vc{k}", [64, 16, 128], BF16) for k in range(2)]
                qin = sbt(es, "qin", [128, PZ], F32)
                Q1 = sbt(es, "Q1", [128, PZ], F32)
                Qt16 = sbt(es, "Qt16", [128, PZ], BF16)
                gin = sbt(es, "gin", [128, PZ], F32)
                G1 = sbt(es, "G1", [128, PZ], F32)
                dec = sbt(es, "dec", [128, 16], F32)
                S = sbt(es, "S", [128, 128], F32)
                S16 = sbt(es, "S16", [128, 16, 128], BF16)
                sc16 = [sbt(es, f"sc16_{k}", [64, 512], BF16) for k in range(2)]
                for hd in range(2):
                    bld.op('dve', lambda e: e.memset(S[:, :], 0.0), [], ['S'])
                    for p in range(NT // PZ):
                        t0 = p * PZ
                        own = True
                        o0 = t0
                        cb = LX['C'] + hd * LX['NPC'] + p
                        th = (t0 % NO) // 1024
                        vk = (hd * (NT // PZ) + p) % 2
                        Vc = Vcs[vk]
                        for c in range(16):
                            gather(f'Vc{vk}', Vc[:, c, :], GBi4, LX['V'] + hd * LX['NCH'] + p * 16 + c, 64, [], [('Vc', vk)])
                        gather('hA', hA[:, :], GCt[0][th][:, :], cb, 128, [], ['hA'])
                        gather('qin', qin[:, :], GCt[1][th][:, :], cb, 128, [], ['qin'])
                        gather('gin', gin[:, :], GCt[2][th][:, :], cb, 128, [], ['gin'])
                        bld.op('act', lambda e: e.activation(out=hB[:, :], in_=hA[:, :], func=AF.Exp, scale=-1.0), ['hA'], ['hB'])
                        bld.op('act', lambda e: e.activation(out=hB[:, :], in_=hB[:, :], func=AF.Ln, bias=1.0), ['hB'], ['hB'])
                        bld.op('act', lambda e: e.activation(out=hB[:, :], in_=hB[:, :], func=AF.Exp, scale=-1.0), ['hB'], ['hB'])
                        bld.op('dve', lambda e, hd=hd: e.tensor_scalar(
                            out=hB[:, :], in0=hB[:, :], scalar1=omlt[:, hd:hd + 1], scalar2=lbt[:, hd:hd + 1],
                            op0=ALU.mult, op1=ALU.add), ['hB', 'omlt', 'lbt'], ['hB'])
                        bld.op('act', lambda e: e.activation(out=hA[:, :], in_=hB[:, :], func=AF.Ln), ['hB'], ['hA'])
                        bld.op('pool', lambda e: e.tensor_scalar(out=hB[:, :], in0=hB[:, :], scalar1=-1.0, scalar2=1.0,
                                                                 op0=ALU.mult, op1=ALU.add), ['hB', 'hA'], ['hB'])
                        bld.op('dve', lambda e: e.tensor_tensor_scan(
                            out=hC[:, :], data0=scanmask[:, 0:PZ], data1=hA[:, :], initial=0.0, op0=ALU.mult, op1=ALU.add),
                            ['scanmask', 'hA'], ['hC'])
                        bld.op('act', lambda e: e.activation(out=hA[:, :], in_=hC[:, :], func=AF.Exp), ['hC'], ['hA'])
                        bld.op('act', lambda e: e.activation(out=hC[:, :], in_=hC[:, :], func=AF.Exp, scale=-1.0), ['hC', 'hA'], ['hC'])
                        bld.op('dve', lambda e: e.tensor_tensor(out=hC[:, :], in0=hB[:, :], in1=hC[:, :], op=ALU.mult),
                               ['hB', 'hC'], ['hC'])
                        bld.op('dve', lambda e: e.tensor_copy(
                            out=dec[:, :], in_=hA[:, :].rearrange("p (c s) -> p c s", s=64)[:, :, 63]), ['hA'], ['dec'])
                        bld.op('act', lambda e: e.activation(out=Kt16[:, :], in_=hC[:, :], func=AF.Copy), ['hC'], ['Kt16'])
                        bld.op('dve', lambda e: e.tensor_tensor(
                            out=KhT[:, :].rearrange("p (c s) -> p c s", s=64), in0=hC[:, :].rearrange("p (c s) -> p c s", s=64),
                            in1=dec[:, 0:16].unsqueeze(2).to_broadcast([128, 16, 64]), op=ALU.mult), ['hC', 'dec'], ['KhT'])
                        for r in range(2):
                            for k in range(8):
                                c = r * 8 + k
                                bld.op('pe', lambda e, c=c, k=k: e.transpose(
                                    out=pst[0:64, k * 128:(k + 1) * 128], in_=KhT[:, c * 64:(c + 1) * 64], identity=ident[:, :]),
                                    ['KhT', 'ident'], ['pst'])
                            bld.op('act', lambda e, r=r: e.activation(
                                out=KhS[:, r * 8:(r + 1) * 8, :], in_=pst[0:64, :].rearrange("p (k d) -> p k d", k=8), func=AF.Copy),
                                ['pst'], ['KhS'])
                        if own:
                            bld.op('act', lambda e: e.activation(out=Q1[:, :], in_=qin[:, :], func=AF.Exp, scale=-1.0), ['qin'], ['Q1'])
                            bld.op('act', lambda e: e.activation(out=Q1[:, :], in_=Q1[:, :], func=AF.Ln, bias=1.0), ['Q1'], ['Q1'])
                            bld.op('act', lambda e: e.activation(out=Q1[:, :], in_=Q1[:, :], func=AF.Exp, scale=-1.0), ['Q1'], ['Q1'])
                            bld.op('pool', lambda e: e.tensor_tensor(out=Q1[:, :], in0=Q1[:, :], in1=qin[:, :], op=ALU.mult),
                                   ['Q1', 'qin'], ['Q1'])
                            bld.op('dve', lambda e: e.tensor_tensor(out=Qt16[:, :], in0=Q1[:, :], in1=hA[:, :], op=ALU.mult),
                                   ['Q1', 'hA'], ['Qt16'])
                            bld.op('act', lambda e: e.activation(out=G1[:, :], in_=gin[:, :], func=AF.Exp, scale=-1.0), ['gin'], ['G1'])
                            bld.op('act', lambda e: e.activation(out=G1[:, :], in_=G1[:, :], func=AF.Ln, bias=1.0), ['G1'], ['G1'])
                            bld.op('act', lambda e: e.activation(out=G1[:, :], in_=G1[:, :], func=AF.Exp, scale=-1.0), ['G1'], ['G1'])
                            bld.op('pool', lambda e: e.tensor_tensor(out=G1[:, :], in0=G1[:, :], in1=gin[:, :], op=ALU.mult),
                                   ['G1', 'gin'], ['G1'])
                        DB = (0, 1, 2, 6)
                        for c in range(16):
                            db = DB[c // 4]
                            bld.op('pe', lambda e, c=c, db=db, Vc=Vc: e.matmul(
                                psf[db][:, (c % 4) * 128:(c % 4 + 1) * 128], lhsT=KhS[:, c, :], rhs=Vc[:, c, :],
                                start=True, stop=True), ['KhS', ('Vc', vk)], [('psf', db)])
                        for c in range(16):
                            db = DB[c // 4]
                            if own:
                                bld.op('dve', lambda e, c=c: e.tensor_copy(out=S16[:, c, :], in_=S[:, :]), ['S'], ['S16'])
                            bld.op('dve', lambda e, c=c, db=db: e.scalar_tensor_tensor(
                                out=S[:, :], in0=S[:, :], scalar=dec[:, c:c + 1], in1=psf[db][:, (c % 4) * 128:(c % 4 + 1) * 128],
                                op0=ALU.mult, op1=ALU.add), ['S', 'dec', ('psf', db)], ['S'])
                        if own:
                            for grp in range(2):
                                for k in range(8):
                                    c = grp * 8 + k
                                    bld.op('pe', lambda e, c=c, k=k, grp=grp: e.matmul(
                                        psf[3 + grp][0:64, k * 64:(k + 1) * 64], lhsT=Kt16[:, c * 64:(c + 1) * 64],
                                        rhs=Qt16[:, c * 64:(c + 1) * 64], start=True, stop=True),
                                        ['Kt16', 'Qt16'], [('psf', 3 + grp)])
                            for grp in range(2):
                                bld.op('dve', lambda e, grp=grp: e.tensor_tensor(
                                    out=sc16[grp][:, :], in0=psf[3 + grp][0:64, :], in1=incl[0:64, :], op=ALU.mult),
                                    [('psf', 3 + grp), 'incl'], [('sc16', grp)])
                            for grp in range(2):
                                for k in range(8):
                                    c = grp * 8 + k
                                    bld.op('pe', lambda e, c=c, k=k, grp=grp, Vc=Vc: e.matmul(
                                        psf[grp][:, k * 64:(k + 1) * 64], lhsT=Vc[:, c, :], rhs=sc16[grp][:, k * 64:(k + 1) * 64],
                                        start=True, stop=False), [('Vc', vk), ('sc16', grp)], [('psf', grp)])
                                    bld.op('pe', lambda e, c=c, k=k, grp=grp: e.matmul(
                                        psf[grp][:, k * 64:(k + 1) * 64], lhsT=S16[:, c, :], rhs=Qt16[:, c * 64:(c + 1) * 64],
                                        start=False, stop=True), ['S16', 'Qt16'], [('psf', grp)])
                            MB = (2, 5)
                            for grp in range(2):
                                bld.op('act', lambda e, grp=grp: e.activation(out=sqb2[grp][:, :], in_=psf[grp][:, :], func=AF.Square),
                                       [('psf', grp)], [('sqb', grp)])
                            for grp in range(2):
                                bld.op('pe', lambda e, grp=grp: e.matmul(psf[MB[grp]][:, :], lhsT=ones128[:, :], rhs=sqb2[grp][:, :],
                                                                         start=True, stop=True),
                                       ['ones128', ('sqb', grp)], [('psf', MB[grp])])
                            for grp in range(2):
                                bld.op('act', lambda e, grp=grp: e.activation(out=rsb2[grp][:, :], in_=psf[MB[grp]][:, :], func=AF.Ln, bias=EPS),
                                       [('psf', MB[grp])], [('rsb', grp)])
                            for grp in range(2):
                                bld.op('act', lambda e, grp=grp: e.activation(out=rsb2[grp][:, :], in_=rsb2[grp][:, :], func=AF.Exp, scale=-0.5),
                                       [('rsb', grp)], [('rsb', grp)])
                            for grp in range(2):
                                bld.op('dve', lambda e, grp=grp: e.tensor_tensor(out=tmpb2[grp][:, :], in0=psf[grp][:, :], in1=rsb2[grp][:, :],
                                                                                 op=ALU.mult),
                                       [('psf', grp), ('rsb', grp)], [('tmpb', grp)])
                            for grp in range(2):
                                bld.op('dve', lambda e, hd=hd, o0=o0, grp=grp: e.scalar_tensor_tensor(
                                    out=ohg[:, hd, o0 + grp * 512: o0 + (grp + 1) * 512], in0=tmpb2[grp][:, :],
                                    scalar=smallt[:, 8 + hd: 9 + hd], in1=G1[:, grp * 512:(grp + 1) * 512],
                                    op0=ALU.mult, op1=ALU.mult), [('tmpb', grp), 'smallt', 'G1'], ['ohg'])
                for hd in range(2):
                    bld.dma('sp', lambda e, hd=hd: e.dma_start(out=S2B[hd * 128:(hd + 1) * 128, :], in_=ohg[:, hd, :]), 'x2b', ['ohg'], [])
                bld.wait_all('pool')
                allgather('e', S2B, G2B)
                bld.barrier(exclude={('d', 'cc_e')})
                bld.flush()
                finish(2)

            with ExitStack() as es:
                uid[0] += 1
                psX = es.enter_context(nc.psum_tensor(f"psX_{uid[0]}", [128, 3, 1024], F32))
                psOT = es.enter_context(nc.psum_tensor(f"psOT_{uid[0]}", [128, 512], F32))
                psM = es.enter_context(nc.psum_tensor(f"psM_{uid[0]}", [128, 512], F32))
                KTh = [sbt(es, "KTh0", [64, NT], BF16)]
                QTh = [sbt(es, "QTh0", [64, NT], BF16)]
                Vatt = sbt(es, "Vatt", [128, NT // 128, 256], BF16)
                Eb = [sbt(es, f"Eb{k}", [128, 1024], F32) for k in range(2)]
                SPb = [sbt(es, f"SPb{k}", [128, 1024], BF16) for k in range(2)]
                A16 = sbt(es, "A16", [128, 512], BF16)
                Wb = [sbt(es, f"Wb{k}", [128, 1024], BF16) for k in range(2)]
                for nb in range(NT // 128):
                    gather('Vatt', Vatt[:, nb, :], GBv2, LX['B'] + nb, 128, [], ['Vatt'])

                def load_head(h):
                    sl = 0
                    for j in range(2):
                        gather(f"KTh{sl}", KTh[sl][:, j * NO:(j + 1) * NO], GAk[:, :], LX['A'] + h * 2 + j, 64, [], [('KTh', sl)])
                        gather(f"QTh{sl}", QTh[sl][:, j * NO:(j + 1) * NO], GAq[:, :], LX['A'] + h * 2 + j, 64, [], [('QTh', sl)])

                def filler(k):
                    for _ in range(k):
                        bld.op('pe', lambda e: e.matmul(psM[:, :], lhsT=ident[:, :], rhs=maskneg[:, 0:512],
                                                        start=True, stop=True), ['ident', 'maskneg'], ['psM'])

                def qk(p, kbs, n, xbase, sl, qs):
                    xk = (xbase + p) % 3
                    for half in range(2):
                        kb = kbs[2 * p + half]
                        j = kb - (n - 4)
                        xo = psX[:, xk, half * 512:(half + 1) * 512]
                        bld.op('pe', lambda e, kb=kb, j=j, xo=xo: e.matmul(
                            xo, lhsT=KTh[sl][:, kb * 128:(kb + 1) * 128], rhs=QTh[sl][:, qs],
                            start=True, stop=(j < 0)), [('KTh', sl), ('QTh', sl)], [('psX', xk)])
                        if j >= 0:
                            bld.op('pe', lambda e, j=j, xo=xo: e.matmul(
                                xo, lhsT=ident[:, :], rhs=maskneg[:, j * 512:(j + 1) * 512],
                                start=False, stop=True), ['ident', 'maskneg'], [('psX', xk)])

                def exp1(p, xbase):
                    xk = (xbase + p) % 3
                    ek = (xbase + p) % 2
                    bld.op('act', lambda e: e.activation(out=Eb[ek][:, :], in_=psX[:, xk, :], func=AF.Exp),
                           [('psX', xk)], [('Eb', ek)])

                def ln2(p, xbase):
                    ek = (xbase + p) % 2
                    spk = (xbase + p) % 2
                    bld.op('act', lambda e: e.activation(out=SPb[spk][:, :], in_=Eb[ek][:, :], func=AF.Ln, bias=1.0),
                           [('Eb', ek)], [('SPb', spk)])

                def cum(p, xbase):
                    xk = (xbase + p) % 3
                    spk = (xbase + p) % 2
                    for half in range(2):
                        first = (p == 0 and half == 0)
                        xo = psX[:, xk, half * 512:(half + 1) * 512]
                        spv = SPb[spk][:, half * 512:(half + 1) * 512]
                        bld.op('pe', lambda e, xo=xo, spv=spv, first=first: e.matmul(
                            xo, lhsT=negU[:, :], rhs=spv, start=False, stop=first, skip_group_check=True),
                            ['negU', ('SPb', spk)], [('psX', xk)])
                        if not first:
                            bld.op('pe', lambda e, xo=xo: e.matmul(
                                xo, lhsT=negOnes[:, :], rhs=A16[:, :], start=False, stop=True, skip_group_check=True),
                                ['negOnes', 'A16'], [('psX', xk)])
                        if first:
                            bld.op('dve', lambda e, spv=spv: e.tensor_copy(out=A16[:, :], in_=spv), [('SPb', spk)], ['A16'])
                        else:
                            bld.op('dve', lambda e, spv=spv: e.tensor_tensor(out=A16[:, :], in0=A16[:, :], in1=spv, op=ALU.add),
                                   ['A16', ('SPb', spk)], ['A16'])

                def expw(p, xbase):
                    xk = (xbase + p) % 3
                    wk = (xbase + p) % 2
                    bld.op('act', lambda e: e.activation(out=Wb[wk][:, :], in_=psX[:, xk, :], func=AF.Exp),
                           [('psX', xk)], [('Wb', wk)])

                def pv(p, kbs, n, xbase, sl, h):
                    wk = (xbase + p) % 2
                    for half in range(2):
                        kb = kbs[2 * p + half]
                        i = 2 * p + half
                        bld.op('pe', lambda e, kb=kb, i=i, half=half: e.matmul(
                            psOT[0:64, :], lhsT=Vatt[:, kb, h * 64:(h + 1) * 64], rhs=Wb[wk][:, half * 512:(half + 1) * 512],
                            start=(i == 0), stop=(i == n - 1)), ['Vatt', ('Wb', wk)], ['psOT'])

                xc = 0
                for h in range(4):
                    sl = 0
                    load_head(h)
                    for qt in range(NT // 512):
                        n = (qt * 512) // 128 + 4
                        kbs = list(range(n - 1, -1, -1))
                        P = n // 2
                        xbase = xc
                        xc += P
                        qs = slice(qt * 512, (qt + 1) * 512)
                        qk(0, kbs, n, xbase, sl, qs)
                        for p in range(P + 2):
                            if 1 <= p <= P:
                                cum(p - 1, xbase)
                            filler(NFILL)
                            if p < P:
                                exp1(p, xbase)
                            if p >= 2:
                                expw(p - 2, xbase)
                            if p + 1 < P:
                                qk(p + 1, kbs, n, xbase, sl, qs)
                            if p >= 2:
                                pv(p - 2, kbs, n, xbase, sl, h)
                            if p < P:
                                ln2(p, xbase)
                        bld.op('act', lambda e: e.activation(out=sqb[0:64, :], in_=psOT[0:64, :], func=AF.Square),
                               ['psOT'], ['sqb'])
                        bld.op('pe', lambda e: e.matmul(psM[0:64, :], lhsT=ones64[:, :], rhs=sqb[0:64, :], start=True, stop=True),
                               ['ones64', 'sqb'], ['psM'])
                        bld.op('act', lambda e: e.activation(out=rsb[0:64, :], in_=psM[0:64, :], func=AF.Ln, bias=EPS),
                               ['psM'], ['rsb'])
                        bld.op('act', lambda e: e.activation(out=rsb[0:64, :], in_=rsb[0:64, :], func=AF.Exp, scale=-0.5),
                               ['rsb'], ['rsb'])
                        bld.op('dve', lambda e: e.tensor_tensor(out=tmpb[0:64, :], in0=psOT[0:64, :], in1=rsb[0:64, :], op=ALU.mult),
                               ['psOT', 'rsb'], ['tmpb'])
                        bld.op('dve', lambda e, h=h, qs=qs: e.tensor_scalar(out=osb[:, h, qs], in0=tmpb[0:64, :], scalar1=smallt[0:64, h:h + 1],
                                                                          scalar2=None, op0=ALU.mult), ['tmpb', 'smallt'], ['osb'])
                    bld.dma('sp', lambda e, h=h: e.dma_start(out=S2A[h * 64:(h + 1) * 64, :], in_=osb[:, h, :]), 'x2a', ['osb'], [])
                bld.barrier()
                bld.flush()
                finish(3)

            for h in range(4):
                if dbg:
                    bld.dma('sp', lambda e, h=h: e.dma_start(out=d_osb[h * 64:(h + 1) * 64, :], in_=osb[:, h, :]), 'dbg2', ['osb'], [])
            for hd in range(2):
                if dbg:
                    bld.dma('sp', lambda e, hd=hd: e.dma_start(out=d_ohg[hd * 128:(hd + 1) * 128, :], in_=ohg[:, hd, :]), 'dbg1', ['ohg'], [])
            bld.barrier()
            allgather('d', S2A, G2A)
            bld.barrier()
            bld.flush()
            finish(4)

        with ExitStack() as es:
            psf, pst = alloc_psum(es)
            wosb = sbt(es, "wosb", [64, 8, D], BF16)
            wohg = sbt(es, "wohg", [128, 4, D], BF16)
            msb = sbt(es, "msb", [64, 8, NO], BF16)
            mhg = sbt(es, "mhg", [128, 4, NO], BF16)
            bld.dma('pool', lambda e: e.dma_start(out=wosb[:, :, :], in_=wout.ap()[0:512, :].rearrange("(h d) m -> d h m", d=64)),
                    'wosb', [], ['wosb'])
            bld.dma('pool', lambda e: e.dma_start(out=wohg[:, :, :], in_=wout.ap()[512:1024, :].rearrange("(h p) m -> p h m", p=128)),
                    'wohg', [], ['wohg'])
            for h in range(8):
                gather('msb', msb[:, h, :], G2Av, LX['E'] + h, 64, [], ['msb'])
            for hd in range(4):
                gather('mhg', mhg[:, hd, :], G2Bv, LX['E'] + 8 + hd, 128, [], ['mhg'])
            yc = 0
            for ti in range(NTILE):
                ts = slice(ti * 128, (ti + 1) * 128)
                for half in range(2):
                    hs = slice(half * 512, (half + 1) * 512)
                    yb = yc % 4
                    yc += 1
                    for h in range(8):
                        bld.op('pe', lambda e, h=h, yb=yb, ts=ts, hs=hs: e.matmul(
                            psf[yb][:, :], lhsT=msb[:, h, ts], rhs=wosb[:, h, hs], start=(h == 0), stop=False),
                            ['msb', 'wosb'], [('psf', yb)])
                    for hd in range(4):
                        bld.op('pe', lambda e, hd=hd, yb=yb, ts=ts, hs=hs: e.matmul(
                            psf[yb][:, :], lhsT=mhg[:, hd, ts], rhs=wohg[:, hd, hs], start=False, stop=(hd == 3)),
                            ['mhg', 'wohg'], [('psf', yb)])
                    bld.op('dve', lambda e, yb=yb, ti=ti, hs=hs: e.tensor_tensor(
                        out=res[:, ti, hs], in0=psf[yb][:, :], in1=res[:, ti, hs], op=ALU.add),
                        [('psf', yb), ('res', ti)], [('res', ti)])
            bld.barrier()
            bld.flush()

        with ExitStack() as es3:
            psf, pst = alloc_psum(es3)
            tokb = alloc_tok(es3)
            ffnb = alloc_ffn(es3)
            ost = [sbt(es3, f"ost{k}", [128, D], F32) for k in range(2)]
            norm_T(tokb, None, 2)
            ffn_groups(tokb, ffnb, w2g, w2u, w2d)
            sqj = tokb[2]
            bld.dma('sp', lambda e: e.dma_start(out=gt[:, :], in_=gains[3, :, :]), 'gt', [], ['gt'])
            for i in range(NTILE):
                bld.op('act', lambda e, i=i: e.activation(out=sqj[:, :], in_=res[:, i, :], func=AF.Square),
                       [('res', i)], ['sqj'])
                bld.op('dve', lambda e, i=i: e.reduce_sum(out=ss[:, i:i + 1], in_=sqj[:, :], axis=AX.X), ['sqj'], ['ss'])
            bld.op('act', lambda e: e.activation(out=rstd[:, 0:NTILE], in_=ss[:, 0:NTILE], func=AF.Ln, scale=1.0 / D, bias=EPS),
                   ['ss'], ['rstd'])
            bld.op('act', lambda e: e.activation(out=rstd[:, 0:NTILE], in_=rstd[:, 0:NTILE], func=AF.Exp, scale=-0.5),
                   ['rstd'], ['rstd'])
            for i in range(NTILE):
                k = i % 2
                bld.op('dve', lambda e, i=i, k=k: e.scalar_tensor_tensor(
                    out=ost[k][:, :], in0=res[:, i, :], scalar=rstd[:, i:i + 1], in1=gt[:, :], op0=ALU.mult, op1=ALU.mult),
                    [('res', i), 'rstd', 'gt'], [('ost', k)])
                bld.dma('sp', lambda e, i=i, k=k: e.dma_start(out=out[i * 128:(i + 1) * 128, :], in_=ost[k][:, :]),
                        f"ost{k}", [('ost', k)], [])
            bld.barrier()
            bld.flush()
    except _Stop:
        pass
    return nc


def make_consts():
    c = np.zeros((128, CW), np.float32)
    c[:, 0:128] = np.eye(128, dtype=np.float32)
    j = np.arange(128)[:, None]
    s = np.arange(128)[None, :]
    c[:, 128:256] = np.where(j >= s, -1.0, 0.0)
    sp = np.arange(128)[:, None]
    t = np.arange(512)[None, :]
    for jj in range(4):
        m = (t < 128 * jj) | ((t < 128 * (jj + 1)) & (sp >= t - 128 * jj))
        c[:, 256 + jj * 512: 256 + (jj + 1) * 512] = np.where(m, NEG, 0.0)
    sm = np.ones(1024, np.float32)
    sm[::64] = 0.0
    c[:, 2304:3328] = sm[None, :]
    s64 = np.arange(64)[:, None]
    t64 = np.arange(64)[None, :]
    inc = np.where(s64 <= t64, 1.0, 0.0).astype(np.float32)
    c[0:64, 3328:3840] = np.tile(inc, (1, 8))
    return c


def make_core_inputs(xo, p, r):
    NO = xo.shape[0]
    small = np.zeros((128, 20), np.float32)
    small[0:64, 0:4] = p["sb_out_norm"].reshape(8, 64).T[:, 4 * r:4 * r + 4]
    small[:, 8:10] = p["hg_out_norm"].reshape(4, 128).T[:, 2 * r:2 * r + 2]
    small[:, 12:14] = p["hg_lower_bound_logits"][0].reshape(4, 128).T[:, 2 * r:2 * r + 2]
    small[:, 16:18] = p["hg_lower_bound_logits"][1].reshape(4, 128).T[:, 2 * r:2 * r + 2]
    gains = np.stack([np.broadcast_to(p[k][None, :], (128, D)) for k in ("ffn1_norm", "mix_norm", "ffn2_norm", "final_norm")])
    return {
        "xin": np.ascontiguousarray(xo, dtype=np.float32),
        "idxd": make_idx(r, NO),
        "gains": np.ascontiguousarray(gains, dtype=np.float32),
        "cst": make_consts(),
        "small": small,
        "w1g": p["ffn1_w_gate"], "w1u": p["ffn1_w_up"], "w1d": p["ffn1_w_down"],
        "w2g": p["ffn2_w_gate"], "w2u": p["ffn2_w_up"], "w2d": p["ffn2_w_down"],
        "win": p["w_in"], "wout": p["w_out"],
    }


def squeeze_params(inputs):
    p = {}
    for k, v in inputs.items():
        if k == "x":
            continue
        v = np.asarray(v, dtype=np.float32)
        if k in ("final_norm", "hg_lower_bound_logits"):
            p[k] = np.ascontiguousarray(v)
        else:
            p[k] = np.ascontiguousarray(v[0])
    return p


def kernel(**inputs):
    x = np.asarray(inputs["x"], dtype=np.float32)
    B, T, _ = x.shape
    H = T // 2
    p = squeeze_params(inputs)
    nc = build(H)
    in_maps = []
    for c in range(8):
        b, hh = c // 2, c % 2
        in_maps.append(make_core_inputs(x[b, hh * H:(hh + 1) * H], p, hh))
    res = run_bass_kernel_spmd(nc, in_maps, core_ids=list(range(8)))
    out = np.zeros((B, T, D), np.float32)
    for c in range(8):
        b, hh = c // 2, c % 2
        out[b, hh * H:(hh + 1) * H] = np.asarray(res.results[c]["out"], dtype=np.float32)
    return out
```
